# Optimizing a Trainium2 kernel written in Bass

```python
import jax, jax.numpy as jnp
from jax import lax
import numpy as np

D_MODEL = 1024
BATCH = 4
SEQ = 8192
DEPTH = 2

N_EVEN = (DEPTH + 1) // 2
N_ODD = DEPTH // 2
CHUNK = 128
CONV_K = 4
NORM_EPS = 1e-6
ROPE_BASE = 10000.0

RET_HEADS = 4
RET_DIM = 128
RET_WIDTH = RET_HEADS * RET_DIM
ML_HEADS = 4
ML_DIM = 128
ML_WIDTH = ML_HEADS * ML_DIM
AB_IN = 4 * RET_WIDTH + 4 * ML_WIDTH + 2 * ML_HEADS
AB_MIX = RET_WIDTH + ML_WIDTH

DN_HEADS = 8
DN_DIM = 128
DN_WIDTH = DN_HEADS * DN_DIM
DN_IN = 4 * DN_WIDTH + 2 * DN_HEADS

PEER_HEADS = 8
PEER_NK = 128
PEER_EXPERTS = PEER_NK * PEER_NK
PEER_TOPK = 16
PEER_QDIM = 256
PEER_HALF = PEER_QDIM // 2
PEER_BLOCK = 128
PEER_V_SCALE = 0.1

PLE_DIM = 256

kernel_name = "hybrid_retention_mlstm_gdn_peer"

F32 = jnp.float32


def rmsnorm(x, w):
    xf = x.astype(F32)
    y = xf * lax.rsqrt(jnp.mean(xf * xf, axis=-1, keepdims=True) + NORM_EPS)
    return (y * w.astype(F32)).astype(x.dtype)


def head_layernorm(y, w):
    mu = jnp.mean(y, axis=-1, keepdims=True)
    yc = y - mu
    yn = yc * lax.rsqrt(jnp.mean(yc * yc, axis=-1, keepdims=True) + NORM_EPS)
    B, S, H, d = y.shape
    return yn.reshape(B, S, H * d) * w.astype(F32)


def l2norm(x):
    return x * lax.rsqrt(jnp.sum(x * x, axis=-1, keepdims=True) + NORM_EPS)


def split_heads(x, n_heads):
    B, S, W = x.shape
    return x.reshape(B, S, n_heads, W // n_heads)


def to_chunks(x):
    B, S, H = x.shape[:3]
    y = x.reshape((B, S // CHUNK, CHUNK, H) + x.shape[3:])
    return y.transpose((1, 0, 3, 2) + tuple(range(4, y.ndim)))


def from_chunks(y):
    n, B, H, C, d = y.shape
    return y.transpose(1, 0, 3, 2, 4).reshape(B, n * C, H, d)


def causal_conv(x, w):
    K, ch = w.shape
    return lax.conv_general_dilated(
        x, w.astype(x.dtype)[:, None, :], window_strides=(1,), padding=[(K - 1, 0)],
        dimension_numbers=("NWC", "WIO", "NWC"), feature_group_count=ch)


def rotary(x):
    S, d = x.shape[1], x.shape[3]
    pos = jnp.arange(S, dtype=F32)
    inv = 1.0 / (ROPE_BASE ** (jnp.arange(0, d, 2, dtype=F32) / d))
    ang = pos[:, None] * inv[None, :]
    cos = jnp.cos(ang)[None, :, None, :]
    sin = jnp.sin(ang)[None, :, None, :]
    x1, x2 = x[..., : d // 2], x[..., d // 2:]
    return jnp.concatenate([x1 * cos - x2 * sin, x2 * cos + x1 * sin], axis=-1)


def retention_chunkwise(q, k, v):
    B, S, H, d = q.shape
    dv = v.shape[-1]
    log_g = jnp.log(1.0 - 2.0 ** (-5.0 - jnp.arange(H, dtype=F32)))
    idx = jnp.arange(CHUNK, dtype=F32)
    diff = idx[:, None] - idx[None, :]
    causal = diff >= 0
    dmat = jnp.where(causal[None], jnp.exp(jnp.where(causal, diff, 0.0)[None] * log_g[:, None, None]), 0.0)
    xi = jnp.exp((idx + 1.0)[None, :] * log_g[:, None])
    zeta = jnp.exp((CHUNK - 1.0 - idx)[None, :] * log_g[:, None])
    g_chunk = jnp.exp(CHUNK * log_g)
    qc, kc, vc = to_chunks(q), to_chunks(k * d ** -0.5), to_chunks(v)

    def step(state, inp):
        qb, kb, vb = inp
        s = jnp.einsum("bhid,bhjd->bhij", qb, kb) * dmat
        o = jnp.einsum("bhij,bhje->bhie", s, vb) + jnp.einsum("bhid,bhde->bhie", qb, state) * xi[:, :, None]
        state = g_chunk[:, None, None] * state + jnp.einsum("bhjd,bhje->bhde", kb * zeta[:, :, None], vb)
        return state, o

    init = jnp.zeros((B, H, d, dv), F32)
    _, o = lax.scan(step, init, (qc, kc, vc))
    return from_chunks(o)


def mlstm_chunkwise(q, k, v, i_pre, f_pre):
    B, S, H, d = q.shape
    dv = v.shape[-1]
    qc, kc, vc = to_chunks(q), to_chunks(k * d ** -0.5), to_chunks(v)
    ic = to_chunks(i_pre)
    lfc = to_chunks(jax.nn.log_sigmoid(f_pre))
    causal = jnp.tril(jnp.ones((CHUNK, CHUNK), dtype=bool))

    def step(carry, inp):
        c_st, n_st, m_st = carry
        qb, kb, vb, ib, lfb = inp
        b = jnp.cumsum(lfb, axis=-1)
        dlog = jnp.where(causal, b[..., :, None] - b[..., None, :] + ib[..., None, :], -jnp.inf)
        inter_log = b + m_st[..., None]
        m_t = jnp.maximum(jnp.max(dlog, axis=-1), inter_log)
        dw = jnp.exp(dlog - m_t[..., None])
        inter_w = jnp.exp(inter_log - m_t)
        s = jnp.einsum("bhid,bhjd->bhij", qb, kb) * dw
        num = jnp.einsum("bhij,bhje->bhie", s, vb) + inter_w[..., None] * jnp.einsum("bhid,bhde->bhie", qb, c_st)
        den = jnp.sum(s, axis=-1) + inter_w * jnp.einsum("bhid,bhd->bhi", qb, n_st)
        h = num / jnp.maximum(jnp.abs(den), jnp.exp(-m_t))[..., None]
        b_last = b[..., -1]
        w_log = b_last[..., None] - b + ib
        m_new = jnp.maximum(b_last + m_st, jnp.max(w_log, axis=-1))
        w = jnp.exp(w_log - m_new[..., None])
        dec = jnp.exp(b_last + m_st - m_new)
        c_st = dec[..., None, None] * c_st + jnp.einsum("bhj,bhjd,bhje->bhde", w, kb, vb)
        n_st = dec[..., None] * n_st + jnp.einsum("bhj,bhjd->bhd", w, kb)
        return (c_st, n_st, m_new), h

    init = (jnp.zeros((B, H, d, dv), F32), jnp.zeros((B, H, d), F32), jnp.zeros((B, H), F32))
    _, h = lax.scan(step, init, (qc, kc, vc, ic, lfc))
    return from_chunks(h)


def gated_delta_chunkwise(q, k, v, g, beta):
    B, S, H, d = q.shape
    dv = v.shape[-1]
    qc = to_chunks(l2norm(q) * d ** -0.5)
    kc = to_chunks(l2norm(k))
    vc = to_chunks(v)
    gc, bc = to_chunks(g), to_chunks(beta)
    G = jnp.cumsum(gc, axis=-1)
    lower = jnp.tril(jnp.ones((CHUNK, CHUNK), dtype=bool))
    strict = jnp.tril(jnp.ones((CHUNK, CHUNK), dtype=bool), k=-1)
    decay = jnp.exp(jnp.where(lower, G[..., :, None] - G[..., None, :], -jnp.inf))
    kb = kc * bc[..., None]
    a_mat = jnp.where(strict, jnp.einsum("nbhid,nbhjd->nbhij", kb, kc) * decay, 0.0)
    eye = jnp.eye(CHUNK, dtype=F32)
    rhs = jnp.concatenate([vc * bc[..., None], kb * jnp.exp(G)[..., None]], axis=-1)
    uw = lax.linalg.triangular_solve(a_mat + eye, rhs, left_side=True, lower=True, unit_diagonal=True)
    u, w = uw[..., :dv], uw[..., dv:]
    attn = jnp.einsum("nbhid,nbhjd->nbhij", qc, kc) * decay
    q_dec = qc * jnp.exp(G)[..., None]
    k_dec = kc * jnp.exp(G[..., -1:] - G)[..., None]
    g_last = jnp.exp(G[..., -1])

    def step(state, inp):
        ub, wb, ab, qd, kd, gl = inp
        v_new = ub - jnp.einsum("bhid,bhde->bhie", wb, state)
        o = jnp.einsum("bhid,bhde->bhie", qd, state) + jnp.einsum("bhij,bhje->bhie", ab, v_new)
        state = gl[..., None, None] * state + jnp.einsum("bhjd,bhje->bhde", kd, v_new)
        return state, o

    init = jnp.zeros((B, H, d, dv), F32)
    _, o = lax.scan(step, init, (u, w, attn, q_dec, k_dec, g_last))
    return from_chunks(o)


def retention_mlstm_mixer(hn, w_in, conv_w, b_i, b_f, gn_ret, gn_mlstm, w_out):
    z = hn @ w_in
    o0 = 0
    r_q = z[..., o0:o0 + RET_WIDTH]; o0 += RET_WIDTH
    r_k = z[..., o0:o0 + RET_WIDTH]; o0 += RET_WIDTH
    r_v = z[..., o0:o0 + RET_WIDTH]; o0 += RET_WIDTH
    r_g = z[..., o0:o0 + RET_WIDTH]; o0 += RET_WIDTH
    m_qk = z[..., o0:o0 + 2 * ML_WIDTH]; o0 += 2 * ML_WIDTH
    m_v = z[..., o0:o0 + ML_WIDTH]; o0 += ML_WIDTH
    m_o = z[..., o0:o0 + ML_WIDTH]; o0 += ML_WIDTH
    m_i = z[..., o0:o0 + ML_HEADS]; o0 += ML_HEADS
    m_f = z[..., o0:o0 + ML_HEADS]

    rq = rotary(split_heads(r_q.astype(F32), RET_HEADS))
    rk = rotary(split_heads(r_k.astype(F32), RET_HEADS))
    ret = retention_chunkwise(rq, rk, split_heads(r_v.astype(F32), RET_HEADS))
    ret = head_layernorm(ret, gn_ret) * jax.nn.silu(r_g.astype(F32))

    qk = jax.nn.silu(causal_conv(m_qk, conv_w).astype(F32))
    mq, mk = qk[..., :ML_WIDTH], qk[..., ML_WIDTH:]
    ml = mlstm_chunkwise(split_heads(mq, ML_HEADS), split_heads(mk, ML_HEADS),
                         split_heads(m_v.astype(F32), ML_HEADS),
                         m_i.astype(F32) + b_i.astype(F32), m_f.astype(F32) + b_f.astype(F32))
    ml = head_layernorm(ml, gn_mlstm) * jax.nn.sigmoid(m_o.astype(F32))

    mixed = jnp.concatenate([ret, ml], axis=-1).astype(hn.dtype)
    return mixed @ w_out


def gated_deltanet_mixer(hn, w_in, conv_w, a_log, dt_bias, norm_w, w_out):
    B, S, _ = hn.shape
    z = hn @ w_in
    qkv = jax.nn.silu(causal_conv(z[..., :3 * DN_WIDTH], conv_w).astype(F32))
    q = split_heads(qkv[..., :DN_WIDTH], DN_HEADS)
    k = split_heads(qkv[..., DN_WIDTH:2 * DN_WIDTH], DN_HEADS)
    v = split_heads(qkv[..., 2 * DN_WIDTH:], DN_HEADS)
    gate = split_heads(z[..., 3 * DN_WIDTH:4 * DN_WIDTH].astype(F32), DN_HEADS)
    beta = jax.nn.sigmoid(z[..., 4 * DN_WIDTH:4 * DN_WIDTH + DN_HEADS].astype(F32))
    a = z[..., 4 * DN_WIDTH + DN_HEADS:].astype(F32)
    g = -jnp.exp(a_log.astype(F32)) * jax.nn.softplus(a + dt_bias.astype(F32))
    o = gated_delta_chunkwise(q, k, v, g, beta)
    o = rmsnorm(o, norm_w) * jax.nn.silu(gate)
    return o.reshape(B, S, DN_WIDTH).astype(hn.dtype) @ w_out


def peer_ffn(hn, w_q, sub_keys, u_tab, v_tab):
    B, S, D = hn.shape
    T = B * S
    xt = hn.reshape(T, D)
    q = (xt @ w_q).astype(F32).reshape(T, PEER_HEADS, 2, PEER_HALF)
    scores = jnp.einsum("thpc,pnc->thpn", q, sub_keys.astype(F32))
    s_top, i_top = lax.top_k(scores, PEER_TOPK)
    cand = (s_top[:, :, 0, :, None] + s_top[:, :, 1, None, :]).reshape(T, PEER_HEADS, PEER_TOPK * PEER_TOPK)
    best, pos = lax.top_k(cand, PEER_TOPK)
    i1 = jnp.take_along_axis(i_top[:, :, 0, :], pos // PEER_TOPK, axis=-1)
    i2 = jnp.take_along_axis(i_top[:, :, 1, :], pos % PEER_TOPK, axis=-1)
    experts = i1 * PEER_NK + i2
    gates = jax.nn.softmax(best, axis=-1)
    nb = T // PEER_BLOCK

    def block(args):
        xb, eb, gb = args
        u = u_tab[eb]
        act = jnp.einsum("thkd,td->thk", u, xb).astype(F32)
        wgt = (gb * jax.nn.gelu(act, approximate=False)).astype(v_tab.dtype)
        return jnp.einsum("thk,thkd->td", wgt, v_tab[eb])

    out = lax.map(block, (xt.reshape(nb, PEER_BLOCK, D),
                          experts.reshape(nb, PEER_BLOCK, PEER_HEADS, PEER_TOPK),
                          gates.reshape(nb, PEER_BLOCK, PEER_HEADS, PEER_TOPK)))
    return out.reshape(B, S, D).astype(hn.dtype)


def per_layer_embedding(h, p_i, w_proj, w_gate, norm_w):
    gate = jax.nn.sigmoid((rmsnorm(h, norm_w) @ w_gate).astype(F32))
    e = (p_i.astype(h.dtype) @ w_proj).astype(F32)
    return (gate * e).astype(h.dtype)


def setup_inputs(seed: int = 0) -> dict:
    key = jax.random.key(seed)
    ks = jax.random.split(key, 32)
    nrm = jax.random.normal
    D = D_MODEL
    dt = jnp.exp(jax.random.uniform(ks[15], (N_ODD, DN_HEADS), minval=np.log(1e-3), maxval=np.log(1e-1)))
    return {
        "x": nrm(ks[0], (BATCH, SEQ, D), F32),
        "p": nrm(ks[1], (DEPTH, BATCH, SEQ, PLE_DIM), F32),
        "norm_mix_w": 1.0 + 0.01 * nrm(ks[2], (DEPTH, D), F32),
        "norm_ffn_w": 1.0 + 0.01 * nrm(ks[3], (DEPTH, D), F32),
        "ab_w_in": nrm(ks[4], (N_EVEN, D, AB_IN), F32) * D ** -0.5,
        "ab_conv_w": nrm(ks[5], (N_EVEN, CONV_K, 2 * ML_WIDTH), F32) * CONV_K ** -0.5,
        "ab_b_i": 0.1 * nrm(ks[6], (N_EVEN, ML_HEADS), F32),
        "ab_b_f": jnp.linspace(3.0, 6.0, ML_HEADS, dtype=F32)[None, :] + 0.1 * nrm(ks[7], (N_EVEN, ML_HEADS), F32),
        "ab_gn_ret": 1.0 + 0.01 * nrm(ks[8], (N_EVEN, RET_WIDTH), F32),
        "ab_gn_mlstm": 1.0 + 0.01 * nrm(ks[9], (N_EVEN, ML_WIDTH), F32),
        "ab_w_out": nrm(ks[10], (N_EVEN, AB_MIX, D), F32) * AB_MIX ** -0.5,
        "dn_w_in": nrm(ks[11], (N_ODD, D, DN_IN), F32) * D ** -0.5,
        "dn_conv_w": nrm(ks[12], (N_ODD, CONV_K, 3 * DN_WIDTH), F32) * CONV_K ** -0.5,
        "dn_a_log": jnp.log(jax.random.uniform(ks[13], (N_ODD, DN_HEADS), minval=1.0, maxval=16.0)),
        "dn_dt_bias": dt + jnp.log(-jnp.expm1(-dt)),
        "dn_norm_w": 1.0 + 0.01 * nrm(ks[14], (N_ODD, DN_DIM), F32),
        "dn_w_out": nrm(ks[16], (N_ODD, DN_WIDTH, D), F32) * DN_WIDTH ** -0.5,
        "peer_w_q": nrm(ks[17], (DEPTH, D, PEER_HEADS * PEER_QDIM), F32) * D ** -0.5,
        "peer_sub_keys": nrm(ks[18], (DEPTH, 2, PEER_NK, PEER_HALF), F32) * PEER_HALF ** -0.5,
        "peer_u": nrm(ks[19], (DEPTH, PEER_EXPERTS, D), F32) * D ** -0.5,
        "peer_v": nrm(ks[20], (DEPTH, PEER_EXPERTS, D), F32) * PEER_V_SCALE,
        "ple_w_proj": nrm(ks[21], (DEPTH, PLE_DIM, D), F32) * PLE_DIM ** -0.5,
        "ple_w_gate": nrm(ks[22], (DEPTH, D, D), F32) * D ** -0.5,
        "ple_norm_w": 1.0 + 0.01 * nrm(ks[23], (DEPTH, D), F32),
        "final_norm_w": 1.0 + 0.01 * nrm(ks[24], (D,), F32),
    }


def reference(x, p, norm_mix_w, norm_ffn_w, ab_w_in, ab_conv_w, ab_b_i, ab_b_f, ab_gn_ret,
              ab_gn_mlstm, ab_w_out, dn_w_in, dn_conv_w, dn_a_log, dn_dt_bias, dn_norm_w,
              dn_w_out, peer_w_q, peer_sub_keys, peer_u, peer_v, ple_w_proj, ple_w_gate,
              ple_norm_w, final_norm_w):
    h = x
    for i in range(DEPTH):
        hn = rmsnorm(h, norm_mix_w[i])
        j = i // 2
        if i % 2 == 0:
            h = h + retention_mlstm_mixer(hn, ab_w_in[j], ab_conv_w[j], ab_b_i[j], ab_b_f[j],
                                          ab_gn_ret[j], ab_gn_mlstm[j], ab_w_out[j])
        else:
            h = h + gated_deltanet_mixer(hn, dn_w_in[j], dn_conv_w[j], dn_a_log[j], dn_dt_bias[j],
                                         dn_norm_w[j], dn_w_out[j])
        h = h + peer_ffn(rmsnorm(h, norm_ffn_w[i]), peer_w_q[i], peer_sub_keys[i], peer_u[i], peer_v[i])
        h = h + per_layer_embedding(h, p[i], ple_w_proj[i], ple_w_gate[i], ple_norm_w[i])
    return rmsnorm(h, final_norm_w)
```

```python
import contextlib
import numpy as np
import concourse.bass as bass
import concourse.mybir as mybir
from concourse.bass_utils import run_bass_kernel_spmd

F32 = mybir.dt.float32
BF16 = mybir.dt.bfloat16
I32 = mybir.dt.int32
U32 = mybir.dt.uint32
AF = mybir.ActivationFunctionType
ALU = mybir.AluOpType
AX = mybir.AxisListType
EPS = 1e-6


class Tile:
    def __init__(self, ctx, name, handle):
        self.ctx = ctx
        self.name = name
        self.h = handle
        self.last_w = []
        self.reads = []
        self.dsem = None
        self.dcount = 0
        self.excl = False

    def __getitem__(self, idx):
        return self.h[idx]


def _compact(lst):
    best = {}
    keep = {}
    for (s, v) in lst:
        k = id(s)
        if k not in best or best[k] < v:
            best[k] = v
            keep[k] = s
    return [(keep[k], best[k]) for k in best]


class Eng:
    def __init__(self, ctx, name, eng):
        self.name = name
        self.eng = eng
        self.sem = ctx.new_sem(name)
        self.count = 0
        self.waited = {}
        self.n_ins = 0
        self.n_wait = 0

    def wait(self, sem, val):
        key = id(sem)
        if self.waited.get(key, 0) >= val:
            return
        self.eng.wait_ge(sem, val)
        self.n_wait += 1
        self.waited[key] = val


class Ctx:
    def __init__(self, nc):
        self.nc = nc
        self.es = contextlib.ExitStack()
        self.nsem = 0
        self.E = {
            "pe": Eng(self, "pe", nc.tensor),
            "act": Eng(self, "act", nc.scalar),
            "dve": Eng(self, "dve", nc.vector),
            "pool": Eng(self, "pool", nc.gpsimd),
            "sp": Eng(self, "sp", nc.sync),
        }
        self._psn = 0
        self.pes = None
        self.semcache = {}
        self.semcount = {}
        self.nalloc = 0

    def new_sem(self, name):
        self.nsem += 1
        return self.es.enter_context(self.nc.semaphore(f"{name}_{self.nsem}"))

    def dma_sem(self, tile):
        key = "d_" + tile.name
        if key not in self.semcache:
            self.semcache[key] = self.new_sem(key)
            self.semcount[key] = 0
        tile.dsem = self.semcache[key]
        tile.dcount = self.semcount[key]
        tile.dkey = key

    def begin_phase(self):
        self.pes = contextlib.ExitStack()

    def barrier(self):
        pend = [(e.sem, e.count) for e in self.E.values() if e.count > 0]
        pend += [(self.semcache[kx], self.semcount[kx]) for kx in self.semcache if self.semcount[kx] > 0]
        for e in self.E.values():
            for (sm, v) in pend:
                e.wait(sm, v)

    def end_phase(self):
        self.barrier()
        self.pes.close()
        self.pes = None

    def sb(self, name, shape, dtype=F32):
        self.nalloc += 1
        st = self.pes if self.pes is not None else self.es
        h = st.enter_context(self.nc.sbuf_tensor(f"{name}_{self.nalloc}", list(shape), dtype))
        return Tile(self, name, h)

    def ps(self, name, shape=(128, 512), dtype=F32):
        h = self.es.enter_context(self.nc.psum_tensor(name, list(shape), dtype))
        t = Tile(self, name, h)
        t.excl = True
        return t

    def dram(self, name, ap):
        return Tile(self, name, ap)

    def op(self, en, fn, reads=(), writes=(), pe_acc=False):
        e = self.E[en]
        for r in reads:
            for (s, v) in r.last_w:
                e.wait(s, v)
            if r.excl:
                for (s, v) in r.reads:
                    if s is not e.sem:
                        e.wait(s, v)
        for w in writes:
            for (s, v) in w.last_w:
                if pe_acc and s is e.sem:
                    continue
                e.wait(s, v)
            for (s, v) in w.reads:
                e.wait(s, v)
        ins = fn()
        e.count += 1
        e.n_ins += 1
        ins.then_inc(e.sem, 1)
        tok = (e.sem, e.count)
        for w in writes:
            w.last_w = [tok]
            w.reads = []
        for r in reads:
            if r in writes:
                continue
            r.reads.append(tok)
            if len(r.reads) > 24:
                r.reads = _compact(r.reads)
        return ins

    def dma(self, en, out_t, in_t, fn, extra_reads=(), par=False):
        e = self.E[en]
        if out_t.dsem is None:
            self.dma_sem(out_t)
        for t in (in_t,) + tuple(extra_reads):
            for (s, v) in t.last_w:
                e.wait(s, v)
        for (s, v) in out_t.last_w:
            if par and s is out_t.dsem:
                continue
            e.wait(s, v)
        for (s, v) in out_t.reads:
            e.wait(s, v)
        ins = fn()
        e.n_ins += 1
        out_t.dcount += 16
        self.semcount[out_t.dkey] = out_t.dcount
        ins.then_inc(out_t.dsem, 16)
        tok = (out_t.dsem, out_t.dcount)
        out_t.last_w = [tok]
        out_t.reads = []
        for t in (in_t,) + tuple(extra_reads):
            t.reads.append(tok)
            if len(t.reads) > 24:
                t.reads = _compact(t.reads)
        return ins

    def finish(self, out_tiles):
        e = self.E["sp"]
        for t in out_tiles:
            for (s, v) in t.last_w:
                e.wait(s, v)

    def stats(self):
        return {k: (v.n_ins, v.n_wait) for k, v in self.E.items()}


class K:
    def __init__(self, nc, c):
        self.nc = nc
        self.c = c

    def mm(self, ps, out, lhsT, rhs, start, stop, reads):
        nc = self.nc
        return self.c.op("pe", lambda: nc.tensor.matmul(out, lhsT=lhsT, rhs=rhs, start=start, stop=stop),
                         reads, [ps], pe_acc=True)

    def tr(self, ps, out, in_, ident, reads):
        nc = self.nc
        return self.c.op("pe", lambda: nc.tensor.transpose(out, in_, ident), reads, [ps], pe_acc=True)

    def act(self, wt, out, in_, func, reads, bias=None, scale=None, accum=None, extra_w=()):
        nc = self.nc
        kw = {}
        if bias is not None:
            kw["bias"] = bias
        if scale is not None:
            kw["scale"] = scale
        if accum is not None:
            kw["accum_out"] = accum
        return self.c.op("act", lambda: nc.scalar.activation(out=out, in_=in_, func=func, **kw),
                         reads, [wt] + list(extra_w))

    def tt(self, en, wt, out, in0, in1, op, reads):
        eng = self.nc.vector if en == "dve" else self.nc.gpsimd
        return self.c.op(en, lambda: eng.tensor_tensor(out=out, in0=in0, in1=in1, op=op), reads, [wt])

    def ts(self, en, wt, out, in0, s1, s2, op0, op1, reads, accum=None, extra_w=()):
        eng = self.nc.vector if en == "dve" else self.nc.gpsimd
        kw = {}
        if accum is not None:
            kw["accum_out"] = accum
        if op1 is None:
            return self.c.op(en, lambda: eng.tensor_scalar(out=out, in0=in0, scalar1=s1, scalar2=None, op0=op0, **kw),
                             reads, [wt] + list(extra_w))
        return self.c.op(en, lambda: eng.tensor_scalar(out=out, in0=in0, scalar1=s1, scalar2=s2, op0=op0, op1=op1, **kw),
                         reads, [wt] + list(extra_w))

    def stt(self, wt, out, in0, scalar, in1, op0, op1, reads, accum=None, extra_w=()):
        nc = self.nc
        kw = {}
        if accum is not None:
            kw["accum_out"] = accum
        return self.c.op("dve", lambda: nc.vector.scalar_tensor_tensor(out=out, in0=in0, scalar=scalar, in1=in1,
                                                                        op0=op0, op1=op1, **kw),
                         reads, [wt] + list(extra_w))

    def cp(self, en, wt, out, in_, reads):
        nc = self.nc
        if en == "act":
            return self.c.op("act", lambda: nc.scalar.activation(out=out, in_=in_, func=AF.Copy), reads, [wt])
        eng = nc.vector if en == "dve" else nc.gpsimd
        return self.c.op(en, lambda: eng.tensor_copy(out=out, in_=in_), reads, [wt])

    def memset(self, en, wt, ap, val):
        eng = self.nc.vector if en == "dve" else self.nc.gpsimd
        return self.c.op(en, lambda: eng.memset(ap, val), [], [wt])

    def recip(self, wt, out, in_, reads):
        nc = self.nc
        return self.c.op("dve", lambda: nc.vector.reciprocal(out=out, in_=in_), reads, [wt])

    def load(self, en, t, out, src_t, in_, par=False):
        eng = {"sp": self.nc.sync, "pool": self.nc.gpsimd, "act": self.nc.scalar}[en]
        if en == "pool":
            return self.c.dma(en, t, src_t, lambda: eng.dma_start(out=out, in_=in_, max_dma_last_dim=2048), par=par)
        return self.c.dma(en, t, src_t, lambda: eng.dma_start(out=out, in_=in_), par=par)

    def rstd_from_ssq(self, ssq, tmp, rstd, n, reads_t):
        self.ts("dve", tmp, tmp[:], ssq[:], 1.0 / n, EPS, ALU.mult, ALU.add, [ssq])
        self.act(tmp, tmp[:], tmp[:], AF.Sqrt, [tmp])
        self.recip(rstd, rstd[:], tmp[:], [tmp])

    def layernorm_rows(self, src_t, src, nrm_t, nrm, st6, mv, tmp, rstd, rms=False):
        nc = self.nc
        self.c.op("dve", lambda: nc.vector.bn_stats(out=st6[:], in_=src), [src_t], [st6])
        self.c.op("dve", lambda: nc.vector.bn_aggr(out=mv[:], in_=st6[:]), [st6], [mv])
        self.ts("dve", tmp, tmp[:], mv[:, 1:2], EPS, None, ALU.add, None, [mv])
        self.act(tmp, tmp[:], tmp[:], AF.Sqrt, [tmp])
        self.recip(rstd, rstd[:], tmp[:], [tmp])
        self.ts("dve", nrm_t, nrm, src, mv[:, 0:1], rstd[:], ALU.subtract, ALU.mult, [src_t, mv, rstd])


class Env:
    def __init__(self, nc=None, c=None, k=None, P=None, tag="", over=None):
        self.fused = nc is not None
        if nc is None:
            nc = bass.Bass("TRN2", target_bir_lowering=False)
            c = Ctx(nc)
            k = K(nc, c)
            P = [c.ps(f"P{i}") for i in range(8)]
        self.nc, self.c, self.k, self.P = nc, c, k, P
        self.tag = tag
        self.over = over or {}
        self.declared = {}

    def din(self, name, shape, dt=F32):
        if name in self.over:
            return self.over[name]
        return self.nc.dram_tensor(self.tag + name, list(shape), dt, kind="ExternalInput").ap()

    def dout(self, name, shape, dt=F32):
        if name in self.over:
            return self.over[name]
        return self.nc.dram_tensor(self.tag + name, list(shape), dt, kind="ExternalOutput").ap()

    def begin(self):
        if self.fused:
            self.c.begin_phase()

    def end(self, outs):
        if self.fused:
            self.c.end_phase()
            return None
        self.c.finish(outs)
        return self.nc


def _din(nc, name, shape, dt=F32):
    return nc.dram_tensor(name, list(shape), dt, kind="ExternalInput").ap()


def _dout(nc, name, shape, dt=F32):
    return nc.dram_tensor(name, list(shape), dt, kind="ExternalOutput").ap()


NTOK0 = 1540


def build_A0(S, env=None):
    env = env or Env()
    nc, c, k = env.nc, env.c, env.k
    env.begin()
    NCH = S // 128
    x = env.din("x", [S, 1024])
    wn = env.din("wn", [128, 1024])
    w_tok = env.din("w_tok", [1024, NTOK0])
    w_feat = env.din("w_feat", [1024, 512])
    convw = env.din("convw", [128, 16])
    bif = env.din("bif", [128, 4])
    gnw = env.din("gnw", [128, 512])
    cosd = env.din("cos", [S, 64])
    sind = env.din("sin", [S, 64])
    dmT = env.din("dmT", [128, 256])
    xir = env.din("xir", [128, 256])
    zg = env.din("zg", [128, 4])
    identf = env.din("identf", [128, 128])
    utri = env.din("utri", [128, 128])
    y = env.dout("y", [S, 512])

    D = {n: c.dram(n, a) for n, a in dict(x=x, wn=wn, w_tok=w_tok, w_feat=w_feat, convw=convw, bif=bif, gnw=gnw,
                                            cos=cosd, sin=sind, dmT=dmT, xir=xir, zg=zg, identf=identf,
                                            utri=utri, y=y).items()}
    wn_t = c.sb("wn_t", [128, 1024])
    wtok_t = c.sb("wtok_t", [128, 8, NTOK0], BF16)
    wfeat_t = c.sb("wfeat_t", [128, 8, 512], BF16)
    convw_t = c.sb("convw_t", [128, 16])
    bif_t = c.sb("bif_t", [128, 4])
    gnw_t = c.sb("gnw_t", [128, 512])
    dmT_t = c.sb("dmT_t", [128, 256])
    xir_t = c.sb("xir_t", [128, 256])
    zg_t = c.sb("zg_t", [128, 4])
    idf_t = c.sb("idf_t", [128, 128])
    idb_t = c.sb("idb_t", [128, 128], BF16)
    utri_t = c.sb("utri_t", [128, 128])
    k.load("sp", wn_t, wn_t[:], D["wn"], wn[:, :])
    for kk in range(8):
        k.load("pool", wtok_t, wtok_t[:, kk, :], D["w_tok"], w_tok[kk * 128:(kk + 1) * 128, :], par=True)
        k.load("pool", wfeat_t, wfeat_t[:, kk, :], D["w_feat"], w_feat[kk * 128:(kk + 1) * 128, :], par=True)
    for t, d, a in [(convw_t, "convw", convw), (bif_t, "bif", bif), (gnw_t, "gnw", gnw), (dmT_t, "dmT", dmT),
                    (xir_t, "xir", xir), (zg_t, "zg", zg), (idf_t, "identf", identf), (utri_t, "utri", utri)]:
        k.load("sp", t, t[:], D[d], a[:, :])
    k.cp("dve", idb_t, idb_t[:], idf_t[:], [idf_t])

    xt = [c.sb(f"xt{i}", [128, 1024]) for i in range(2)]
    cs_t = [c.sb(f"cs{i}", [128, 2, 64]) for i in range(2)]
    sq = c.sb("sq", [128, 1024])
    ssq = c.sb("ssq", [128, 1]); tmp1 = c.sb("tmp1", [128, 1]); rstd = c.sb("rstd", [128, 1])
    hn = c.sb("hn", [128, 1024], BF16)
    hnT = c.sb("hnT", [128, 8, 128], BF16)
    ztok = c.sb("ztok", [128, NTOK0])
    zc = c.sb("zc", [128, 4, 131])
    cacc = c.sb("cacc", [128, 4, 128])
    csil = c.sb("csil", [128, 4, 128])
    mqT = c.sb("mqT", [128, 2, 128], BF16)
    mkT = c.sb("mkT", [128, 2, 128], BF16)
    rt1 = c.sb("rt1", [128, 4, 64]); rt2 = c.sb("rt2", [128, 4, 64])
    rot = c.sb("rot", [128, 4, 128], BF16)
    rT = c.sb("rT", [128, 4, 128], BF16)
    rkz = c.sb("rkz", [128, 2, 128], BF16)
    vb = c.sb("vb", [128, 2, 128], BF16)
    St = c.sb("St", [128, 128], BF16)
    rqx = c.sb("rqx", [128, 128], BF16)
    rstate = [c.sb(f"rstate{h}", [128, 128]) for h in range(2)]
    rstate_b = [c.sb(f"rstateb{h}", [128, 128], BF16) for h in range(2)]
    st6 = c.sb("st6", [128, 6]); mv = c.sb("mv", [128, 2]); tmp2 = c.sb("tmp2", [128, 1]); rstd2 = c.sb("rstd2", [128, 1])
    yn = c.sb("yn", [128, 128])
    sg = c.sb("sg", [128, 512])
    yt = [c.sb(f"yt{i}", [128, 512]) for i in range(2)]
    ydst = [c.dram(f"ydst{i}", None) for i in range(2)]
    gpre = c.sb("gpre", [128, 4])
    lf = c.sb("lf", [128, 2]); lft = c.sb("lft", [128, 2])
    lfbc = c.sb("lfbc", [128, 128])
    bcol = c.sb("bcol", [128, 2]); sj = c.sb("sj", [128, 2])
    DT = c.sb("DT", [128, 128]); DTm = c.sb("DTm", [128, 128])
    EB = c.sb("EB", [128, 128])
    blast = c.sb("blast", [128, 1]); wj = c.sb("wj", [128, 1])
    Sd = c.sb("Sd", [128, 128], BF16)
    mqx = c.sb("mqx", [128, 128], BF16)
    vaug = [c.sb(f"vaug{h}", [128, 130], BF16) for h in range(2)]
    kw_t = c.sb("kw_t", [128, 128], BF16)
    Cst = [c.sb(f"Cst{h}", [128, 130]) for h in range(2)]
    Cst_b = [c.sb(f"Cstb{h}", [128, 130], BF16) for h in range(2)]
    den = c.sb("den", [128, 1]); rden = c.sb("rden", [128, 1])
    hh = c.sb("hh", [128, 128])
    P = env.P

    def nps():
        c._psn += 1
        return P[c._psn % 8]

    for h in range(2):
        k.memset("dve", rstate[h], rstate[h][:], 0.0)
        k.memset("dve", rstate_b[h], rstate_b[h][:], 0.0)
        k.memset("dve", Cst[h], Cst[h][:], 0.0)
        k.memset("dve", Cst_b[h], Cst_b[h][:], 0.0)
        k.memset("dve", vaug[h], vaug[h][:], 1.0)
    k.memset("dve", zc, zc[:], 0.0)

    for ch in range(NCH):
        X = xt[ch % 2]; CS = cs_t[ch % 2]; Y = yt[ch % 2]
        r0 = ch * 128
        k.load("sp", X, X[:], D["x"], x[r0:r0 + 128, :])
        k.load("sp", CS, CS[:, 0, :], D["cos"], cosd[r0:r0 + 128, :], par=True)
        k.load("sp", CS, CS[:, 1, :], D["sin"], sind[r0:r0 + 128, :], par=True)
        k.act(sq, sq[:], X[:], AF.Square, [X], accum=ssq[:], extra_w=[ssq])
        k.rstd_from_ssq(ssq, tmp1, rstd, 1024.0, None)
        k.stt(hn, hn[:], X[:], rstd[:], wn_t[:], ALU.mult, ALU.mult, [X, rstd, wn_t])
        pt = nps()
        ptb = pt[:].bitcast(BF16)
        for kk in range(8):
            k.tr(pt, ptb[:, kk * 128:(kk + 1) * 128], hn[:, kk * 128:(kk + 1) * 128], idb_t[:], [hn, idb_t])
        k.cp("act", hnT, hnT[:].rearrange("p a b -> p (a b)"), ptb[:, 0:1024], [pt])
        for nb in range(4):
            c0 = nb * 512
            n = min(512, NTOK0 - c0)
            pz = nps()
            for kk in range(8):
                k.mm(pz, pz[:, 0:n], hnT[:, kk, :], wtok_t[:, kk, c0:c0 + n], kk == 0, kk == 7, [hnT, wtok_t])
            if nb % 2 == 0:
                k.cp("act", ztok, ztok[:, c0:c0 + n], pz[:, 0:n], [pz])
            else:
                k.cp("dve", ztok, ztok[:, c0:c0 + n], pz[:, 0:n], [pz])
        pf = nps()
        for j in range(4):
            for kk in range(8):
                k.mm(pf, pf[:, j * 128:(j + 1) * 128], wfeat_t[:, kk, j * 128:(j + 1) * 128], hnT[:, kk, :],
                     kk == 0, kk == 7, [hnT, wfeat_t])
        k.cp("act", zc, zc[:, :, 3:131], pf[:].rearrange("p (a b) -> p a b", a=4), [pf])
        for j in range(4):
            k.ts("dve", cacc, cacc[:, j, :], zc[:, j, 0:128], convw_t[:, j * 4:j * 4 + 1], None, ALU.mult, None,
                 [zc, convw_t])
            for tp in range(1, 4):
                k.stt(cacc, cacc[:, j, :], zc[:, j, tp:tp + 128], convw_t[:, j * 4 + tp:j * 4 + tp + 1], cacc[:, j, :],
                      ALU.mult, ALU.add, [zc, convw_t, cacc])
        k.cp("dve", zc, zc[:, :, 0:3], zc[:, :, 128:131], [zc])
        k.act(csil, csil[:], cacc[:], AF.Silu, [cacc])
        k.cp("dve", mqT, mqT[:], csil[:, 0:2, :], [csil])
        k.ts("dve", mkT, mkT[:], csil[:, 2:4, :], 128.0 ** -0.5, None, ALU.mult, None, [csil])
        zq = ztok[:, 0:512].rearrange("p (g t d) -> p g t d", g=4, t=2)
        cosb = CS[:, 0:1, :].to_broadcast([128, 4, 64])
        sinb = CS[:, 1:2, :].to_broadcast([128, 4, 64])
        k.tt("dve", rt1, rt1[:], zq[:, :, 0, :], cosb, ALU.mult, [ztok, CS])
        k.tt("pool", rt2, rt2[:], zq[:, :, 1, :], sinb, ALU.mult, [ztok, CS])
        k.tt("dve", rot, rot[:, :, 0:64], rt1[:], rt2[:], ALU.subtract, [rt1, rt2])
        k.tt("pool", rt1, rt1[:], zq[:, :, 1, :], cosb, ALU.mult, [ztok, CS])
        k.tt("dve", rt2, rt2[:], zq[:, :, 0, :], sinb, ALU.mult, [ztok, CS])
        k.tt("dve", rot, rot[:, :, 64:128], rt1[:], rt2[:], ALU.add, [rt1, rt2, rot])
        pr = nps()
        prb = pr[:].bitcast(BF16)
        for g4 in range(4):
            k.tr(pr, prb[:, g4 * 128:(g4 + 1) * 128], rot[:, g4, :], idb_t[:], [rot, idb_t])
        k.cp("act", rT, rT[:].rearrange("p a b -> p (a b)"), prb[:, 0:512], [pr])
        for h in range(2):
            k.ts("dve", rkz, rkz[:, h, :], rot[:, 2 + h, :], zg_t[:, h:h + 1], None, ALU.mult, None, [rot, zg_t])
        k.cp("dve", vb, vb[:].rearrange("p a b -> p (a b)"), ztok[:, 512:768], [ztok])
        k.act(sg, sg[:, 0:256], ztok[:, 768:1024], AF.Silu, [ztok])
        k.act(sg, sg[:, 256:512], ztok[:, 1280:1536], AF.Sigmoid, [ztok, sg])
        for h in range(2):
            pS = nps()
            k.mm(pS, pS[:, 0:128], rT[:, 2 + h, :], rT[:, h, :], True, True, [rT])
            k.tt("dve", St, St[:], pS[:, 0:128], dmT_t[:, h * 128:(h + 1) * 128], ALU.mult, [pS, dmT_t])
            k.tt("pool", rqx, rqx[:], rT[:, h, :], xir_t[:, h * 128:(h + 1) * 128], ALU.mult, [rT, xir_t])
            pO = nps()
            k.mm(pO, pO[:, 0:128], St[:], vb[:, h, :], True, False, [St, vb])
            k.mm(pO, pO[:, 0:128], rqx[:], rstate_b[h][:], False, True, [rqx, rstate_b[h]])
            pU = nps()
            k.mm(pU, pU[:, 0:128], rkz[:, h, :], vb[:, h, :], True, True, [rkz, vb])
            k.stt(rstate[h], rstate[h][:], rstate[h][:], zg_t[:, 2 + h:3 + h], pU[:, 0:128], ALU.mult, ALU.add,
                  [rstate[h], zg_t, pU])
            k.cp("act", rstate_b[h], rstate_b[h][:], rstate[h][:], [rstate[h]])
            k.layernorm_rows(pO, pO[:, 0:128], yn, yn[:], st6, mv, tmp2, rstd2)
            k.tt("dve", yn, yn[:], yn[:], gnw_t[:, h * 128:(h + 1) * 128], ALU.mult, [yn, gnw_t])
            k.tt("dve", Y, Y[:, h * 128:(h + 1) * 128], yn[:], sg[:, h * 128:(h + 1) * 128], ALU.mult, [yn, sg])
        k.tt("dve", gpre, gpre[:], ztok[:, 1536:1540], bif_t[:], ALU.add, [ztok, bif_t])
        k.act(lft, lft[:], gpre[:, 2:4], AF.Exp, [gpre], scale=-1.0)
        k.act(lft, lft[:], lft[:], AF.Ln, [lft], bias=1.0)
        k.ts("dve", lf, lf[:], lft[:], -1.0, None, ALU.mult, None, [lft])
        pb = nps()
        k.mm(pb, pb[:, 0:2], utri_t[:], lf[:], True, True, [utri_t, lf])
        k.cp("dve", bcol, bcol[:], pb[:, 0:2], [pb])
        k.tt("dve", sj, sj[:], gpre[:, 0:2], bcol[:], ALU.subtract, [gpre, bcol])
        for h in range(2):
            k.cp("dve", lfbc, lfbc[:], lf[:, h:h + 1].to_broadcast([128, 128]), [lf])
            pB = nps()
            k.mm(pB, pB[:, 0:128], lfbc[:], utri_t[:], True, True, [lfbc, utri_t])
            k.act(DT, DT[:], pB[:, 0:128], AF.Exp, [pB, sj], bias=sj[:, h:h + 1])
            c.op("pool", lambda: nc.gpsimd.affine_select(out=DTm[:], in_=DT[:], pattern=[[1, 128]],
                                                          compare_op=ALU.is_ge, fill=0.0, base=0,
                                                          channel_multiplier=-1), [DT], [DTm])
            k.act(EB, EB[:], pB[:, 0:128], AF.Exp, [pB])
            k.cp("dve", blast, blast[:], pB[:, 127:128], [pB])
            k.act(wj, wj[:], sj[:, h:h + 1], AF.Exp, [sj, blast], bias=blast[:])
            pS = nps()
            k.mm(pS, pS[:, 0:128], mkT[:, h, :], mqT[:, h, :], True, True, [mkT, mqT])
            k.tt("dve", Sd, Sd[:], pS[:, 0:128], DTm[:], ALU.mult, [pS, DTm])
            k.tt("pool", mqx, mqx[:], mqT[:, h, :], EB[:], ALU.mult, [mqT, EB])
            k.cp("act", vaug[h], vaug[h][:, 0:128], ztok[:, 1024 + h * 128:1024 + (h + 1) * 128], [ztok])
            pN = nps()
            k.mm(pN, pN[:, 0:130], Sd[:], vaug[h][:], True, False, [Sd, vaug[h]])
            k.mm(pN, pN[:, 0:130], mqx[:], Cst_b[h][:], False, True, [mqx, Cst_b[h]])
            pk = nps()
            pkb = pk[:].bitcast(BF16)
            k.tr(pk, pkb[:, 0:128], mkT[:, h, :], idb_t[:], [mkT, idb_t])
            k.ts("dve", kw_t, kw_t[:], pkb[:, 0:128], wj[:], None, ALU.mult, None, [pk, wj])
            pC = nps()
            k.mm(pC, pC[:, 0:130], kw_t[:], vaug[h][:], True, True, [kw_t, vaug[h]])
            k.stt(Cst[h], Cst[h][:], Cst[h][:], EB[:, 127:128], pC[:, 0:130], ALU.mult, ALU.add, [Cst[h], EB, pC])
            k.cp("act", Cst_b[h], Cst_b[h][:], Cst[h][:], [Cst[h]])
            k.act(den, den[:], pN[:, 128:129], AF.Abs, [pN])
            k.ts("dve", den, den[:], den[:], 1.0, None, ALU.max, None, [den])
            k.recip(rden, rden[:], den[:], [den])
            k.ts("dve", hh, hh[:], pN[:, 0:128], rden[:], None, ALU.mult, None, [pN, rden])
            k.layernorm_rows(hh, hh[:], yn, yn[:], st6, mv, tmp2, rstd2)
            k.tt("dve", yn, yn[:], yn[:], gnw_t[:, 256 + h * 128:256 + (h + 1) * 128], ALU.mult, [yn, gnw_t])
            k.tt("dve", Y, Y[:, 256 + h * 128:256 + (h + 1) * 128], yn[:], sg[:, 256 + h * 128:256 + (h + 1) * 128],
                 ALU.mult, [yn, sg])
        c.dma("sp", ydst[ch % 2], Y, lambda: nc.sync.dma_start(out=y[r0:r0 + 128, :], in_=Y[:]))
    return env.end(ydst)


def consts_A0():
    H = 4
    log_g = np.log(1.0 - 2.0 ** (-5.0 - np.arange(H, dtype=np.float32))).astype(np.float32)
    idx = np.arange(128, dtype=np.float32)
    diff = idx[:, None] - idx[None, :]
    causal = diff >= 0
    dmat = np.where(causal[None], np.exp(np.where(causal, diff, 0.0)[None] * log_g[:, None, None]), 0.0).astype(np.float32)
    xi = np.exp((idx + 1.0)[None, :] * log_g[:, None]).astype(np.float32)
    zeta = np.exp((128 - 1.0 - idx)[None, :] * log_g[:, None]).astype(np.float32)
    g_chunk = np.exp(128 * log_g).astype(np.float32)
    return dmat, xi, zeta, g_chunk


def rope_tables(S):
    d = 128
    pos = np.arange(S, dtype=np.float32)
    inv = (1.0 / (np.float32(10000.0) ** (np.arange(0, d, 2, dtype=np.float32) / np.float32(d)))).astype(np.float32)
    ang = (pos[:, None] * inv[None, :]).astype(np.float32)
    return np.cos(ang).astype(np.float32), np.sin(ang).astype(np.float32)


def inputs_A0(x, norm_w, w_in, conv_w, b_i, b_f, gn_ret, gn_ml):
    B, S, _ = x.shape
    dmat, xi, zeta, g_chunk = consts_A0()
    cos, sin = rope_tables(S)
    sc = np.float32(128.0 ** -0.5)
    identf = np.eye(128, dtype=np.float32)
    utri = np.triu(np.ones((128, 128), dtype=np.float32))
    maps = []
    for core in range(8):
        b, g = core // 2, core % 2
        hs = [2 * g, 2 * g + 1]
        def cols(base, width=128):
            return np.concatenate([np.arange(base + h * width, base + (h + 1) * width) for h in hs])
        tok_cols = np.concatenate([cols(0), cols(512), cols(1024), cols(1536), cols(3072), cols(3584),
                                   np.array([4096 + h for h in hs]), np.array([4100 + h for h in hs])])
        feat_cols = np.concatenate([cols(2048), cols(2560)])
        cw = conv_w[:, feat_cols - 2048]
        convw = np.ascontiguousarray(cw.reshape(4, 4, 128).transpose(2, 1, 0).reshape(128, 16))
        bif = np.tile(np.concatenate([b_i[hs], b_f[hs]])[None, :], (128, 1)).astype(np.float32)
        gnw = np.tile(np.concatenate([gn_ret[cols(0)], gn_ml[cols(0)]])[None, :], (128, 1)).astype(np.float32)
        dmT = np.concatenate([dmat[h].T * sc for h in hs], axis=1).astype(np.float32)
        xir = np.concatenate([np.tile(xi[h][None, :], (128, 1)) for h in hs], axis=1).astype(np.float32)
        zg = np.stack([zeta[hs[0]] * sc, zeta[hs[1]] * sc, np.full(128, g_chunk[hs[0]]), np.full(128, g_chunk[hs[1]])],
                      axis=1).astype(np.float32)
        maps.append(dict(
            x=np.ascontiguousarray(x[b]), wn=np.tile(norm_w[None, :], (128, 1)).astype(np.float32),
            w_tok=np.ascontiguousarray(w_in[:, tok_cols]), w_feat=np.ascontiguousarray(w_in[:, feat_cols]),
            convw=convw, bif=bif, gnw=gnw, cos=cos, sin=sin, dmT=np.ascontiguousarray(dmT), xir=np.ascontiguousarray(xir),
            zg=np.ascontiguousarray(zg), identf=identf, utri=utri))
    return maps


def gather_A0(results, B, S):
    mixed = np.empty((B, S, 1024), dtype=np.float32)
    for core in range(8):
        b, g = core // 2, core % 2
        yv = results[core]["y"]
        for j, h in enumerate([2 * g, 2 * g + 1]):
            mixed[b, :, h * 128:(h + 1) * 128] = yv[:, j * 128:(j + 1) * 128]
            mixed[b, :, 512 + h * 128:512 + (h + 1) * 128] = yv[:, 256 + j * 128:256 + (j + 1) * 128]
    return mixed


TB = 256
SCENG = ['dve', 'act']


def build_B(NTOK, final, NI1=128, dbg=99, preconv=True, env=None, write_y=True, write_yf=True):
    env = env or Env()
    nc, c, k = env.nc, env.c, env.k
    env.begin()
    NB = NTOK // TB
    hin = env.din("hin", [NTOK, 1024])
    mixed = env.din("mixed", [NTOK, 1024])
    pin = env.din("pin", [NTOK, 256])
    wout = env.din("wout", [2, 128, 8, 512])
    wgate = env.din("wgate", [2, 128, 8, 512])
    wq = env.din("wq", [16, 128, 8, 128])
    wproj = env.din("wproj", [128, 2, 1024])
    wnorm = env.din("wnorm", [128, 3, 1024])
    skT = env.din("skT", [128, 2, 128])
    ul = env.din("ul", [NI1, 128, 8 * 128])
    vl = env.din("vl", [NI1, 128, 1024])
    identf = env.din("identf", [128, 128])
    iota = env.din("iota", [128, 128])
    y = env.dout("y", [NTOK, 1024]) if write_y else None
    yf = env.dout("yf", [NTOK, 1024]) if write_yf else None
    if "ub" in env.over:
        ub, vbd = env.over["ub"], env.over["vbd"]
    else:
        ub = nc.dram_tensor("ub", [NI1, 128, 1024], BF16, kind="Internal").ap()
        vbd = nc.dram_tensor("vbd", [NI1, 128, 1024], BF16, kind="Internal").ap()
    D = {n: c.dram(n, a) for n, a in dict(hin=hin, mixed=mixed, pin=pin, wout=wout, wgate=wgate, wq=wq, wproj=wproj,
                                            wnorm=wnorm, skT=skT, ul=ul, vl=vl, identf=identf, iota=iota, y=y,
                                            yf=yf, ub=ub, vbd=vbd).items()}
    wn_t = c.sb("wn_t", [128, 3, 1024])
    wproj_t = c.sb("wproj_t", [128, 2, 1024], BF16)
    skT_t = c.sb("skT_t", [128, 2, 128], BF16)
    idf_t = c.sb("idf_t", [128, 128])
    idb_t = c.sb("idb_t", [128, 128], BF16)
    iota_t = c.sb("iota_t", [128, 128])
    k.load("sp", wn_t, wn_t[:], D["wnorm"], wnorm[:, :, :])
    k.load("pool", wproj_t, wproj_t[:], D["wproj"], wproj[:, :, :])
    k.load("pool", skT_t, skT_t[:], D["skT"], skT[:, :, :])
    k.load("sp", idf_t, idf_t[:], D["identf"], identf[:, :])
    k.load("sp", iota_t, iota_t[:], D["iota"], iota[:, :])
    k.cp("dve", idb_t, idb_t[:], idf_t[:], [idf_t])
    uc = [c.sb(f"uc{i}", [128, 8, 128], BF16) for i in range(3)]
    vc = [c.sb(f"vc{i}", [128, 1024], BF16) for i in range(3)]
    stg_main = [uc[i] for i in range(3)] + [vc[i] for i in range(3)]
    stg = [(Tile(c, f"stgU{i}", uc[i].h), uc[i][:].rearrange("p a b -> p (a b)")) for i in range(3)] + \
          [(Tile(c, f"stgV{i}", vc[i].h), vc[i][:]) for i in range(3)]
    stq = [c.dram(f"stq{i}", None) for i in range(6)]
    tabt = {"ub": [], "vbd": []}
    n = 0
    for src, srcn, dst, dstn in [(ul, "ul", ub, "ub"), (vl, "vl", vbd, "vbd")]:
        for i1 in range(NI1 if preconv else 0):
            (s_, sap) = stg[n % 6]; q_ = stq[n % 6]; n += 1
            k.load("pool", s_, sap, D[srcn], src[i1, :, :])
            c.dma("sp", q_, s_, (lambda dst=dst, i1=i1, sap=sap: nc.sync.dma_start(out=dst[i1, :, :], in_=sap)))
            tt_ = c.dram(f"{dstn}{i1}", None)
            tt_.last_w = list(q_.last_w)
            tabt[dstn].append(tt_)
    wA = [c.sb(f"wA{i}", [128, 8, 512], BF16) for i in range(2)]
    wQ = [c.sb(f"wQ{i}", [128, 8, 128], BF16) for i in range(2)]
    h1s = [c.sb(f"h1_{i}", [128, 2, 1024]) for i in range(2)]
    mx = c.sb("mx", [128, 1024])
    mb = c.sb("mb", [128, 1024], BF16)
    mT = c.sb("mT", [128, 8, 128], BF16)
    sq = c.sb("sq", [128, 1024], BF16)
    ssq = c.sb("ssq", [128, 1]); tmp1 = c.sb("tmp1", [128, 1]); rstd = c.sb("rstd", [128, 1])
    hnb = c.sb("hnb", [128, 1024], BF16)
    hnTs = [c.sb(f"hnT{i}", [128, 8, TB], BF16) for i in range(2)]
    qTj = [c.sb(f"qTj{i}", [128, TB], BF16) for i in range(2)]
    sc = [c.sb(f"sc{i}", [128, 16, 128]) for i in range(2)]
    scw = c.sb("scw", [128, 128])
    top = c.sb("top", [128, 16, 16])
    itu = c.sb("itu", [128, 16, 16], U32)
    itf = c.sb("itf", [128, 16, 16])
    cand = c.sb("cand", [128, 8, 256])
    candw = c.sb("candw", [128, 256])
    eful = c.sb("eful", [128, 8, 256])
    best = c.sb("best", [128, 8, 16])
    junk = c.sb("junk", [128, 256])
    eid = c.sb("eid", [128, 128])
    eii = c.sb("eii", [128, 128], I32)
    ei1 = c.sb("ei1", [128, 128], I32); ei2 = c.sb("ei2", [128, 128], I32)
    a1 = c.sb("a1", [128, 128]); a2 = c.sb("a2", [128, 128])
    gat = c.sb("gat", [128, 8, 16]); gsum = c.sb("gsum", [128, 8]); grs = c.sb("grs", [128, 8])
    a1T = c.sb("a1T", [128, TB]); a2T = c.sb("a2T", [128, TB]); gT = c.sb("gT", [128, TB])
    A2 = [c.sb(f"A2_{i}", [128, 128], BF16) for i in range(2)]
    A1 = [c.sb(f"A1_{i}", [128, 128], BF16) for i in range(2)]
    WT = c.sb("WT", [128, 128, TB], BF16)
    ge = [c.sb(f"ge{i}", [128, TB]) for i in range(2)]
    Gt = [c.sb(f"Gt{i}", [128, TB], BF16) for i in range(2)]
    for (al, _), mt in zip(stg, stg_main):
        mt.last_w = list(al.last_w)
        mt.reads = list(al.reads)
    pt_in = c.sb("pt_in", [128, 256]); ptb = c.sb("ptb", [128, 256], BF16); pT = c.sb("pT", [128, 2, 128], BF16)
    sig = c.sb("sig", [128, 512]); tmpe = c.sb("tmpe", [128, 512])
    P = env.P
    PO = P[0:4]
    PA = P[4:6]
    PR = P[6:8]

    def nps():
        c._psn += 1
        return PR[c._psn % 2]

    def rmsnorm_to_T(src_t, src, widx, tt, hnT):
        k.act(sq, sq[:], src, AF.Square, [src_t], accum=ssq[:], extra_w=[ssq])
        k.rstd_from_ssq(ssq, tmp1, rstd, 1024.0, None)
        k.stt(hnb, hnb[:], src, rstd[:], wn_t[:, widx, :], ALU.mult, ALU.mult, [src_t, rstd, wn_t])
        pt = nps()
        ptv = pt[:].bitcast(BF16)
        for kk in range(8):
            k.tr(pt, ptv[:, kk * 128:(kk + 1) * 128], hnb[:, kk * 128:(kk + 1) * 128], idb_t[:], [hnb, idb_t])
        k.cp("act", hnT, hnT[:, :, tt * 128:(tt + 1) * 128], ptv[:, 0:1024].rearrange("p (a b) -> p a b", a=8), [pt])

    def _dbg_out(blk):
        for tt in range(2):
            r0 = blk * TB + tt * 128
            k.cp("act", mx, mx[:], h1[:, tt, :], [h1])
            c.dma("sp", D["y"], mx, (lambda r0=r0: nc.sync.dma_start(out=y[r0:r0 + 128, :], in_=mx[:])), par=True)

    wa_n = [0]

    def prep(blk):
        t0 = blk * TB
        h1 = h1s[blk % 2]; hnT = hnTs[blk % 2]
        for tt in range(2):
            r0 = t0 + tt * 128
            k.load("sp", h1, h1[:, tt, :], D["hin"], hin[r0:r0 + 128, :], par=(tt == 1))
        for tt in range(2):
            r0 = t0 + tt * 128
            k.load("sp", mx, mx[:], D["mixed"], mixed[r0:r0 + 128, :])
            k.cp("dve", mb, mb[:], mx[:], [mx])
            pt = nps()
            ptv = pt[:].bitcast(BF16)
            for kk in range(8):
                k.tr(pt, ptv[:, kk * 128:(kk + 1) * 128], mb[:, kk * 128:(kk + 1) * 128], idb_t[:], [mb, idb_t])
            k.cp("act", mT, mT[:].rearrange("p a b -> p (a b)"), ptv[:, 0:1024], [pt])
            for half in range(2):
                W = wA[wa_n[0] % 2]; wa_n[0] += 1
                k.load("pool", W, W[:], D["wout"], wout[half, :, :, :])
                pz = nps()
                for kk in range(8):
                    k.mm(pz, pz[:, :], mT[:, kk, :], W[:, kk, :], kk == 0, kk == 7, [mT, W])
                k.tt("dve", h1, h1[:, tt, half * 512:(half + 1) * 512], pz[:, :], h1[:, tt, half * 512:(half + 1) * 512],
                     ALU.add, [pz, h1])
                yield
        for tt in range(2):
            rmsnorm_to_T(h1, h1[:, tt, :], 0, tt, hnT)
            yield
        for j in range(16):
            Wj = wQ[j % 2]
            k.load("pool", Wj, Wj[:], D["wq"], wq[j, :, :, :])
            pq = nps()
            for kk in range(8):
                k.mm(pq, pq[:, 0:TB], Wj[:, kk, :], hnT[:, kk, :], kk == 0, kk == 7, [Wj, hnT])
            Q = qTj[j % 2]
            k.cp("act", Q, Q[:], pq[:, 0:TB], [pq])
            psc = nps()
            for tt in range(2):
                k.mm(psc, psc[:, tt * 128:(tt + 1) * 128], Q[:, tt * 128:(tt + 1) * 128], skT_t[:, j % 2, :], True, True,
                     [Q, skT_t])
            for tt in range(2):
                k.cp(SCENG[tt], sc[tt], sc[tt][:, j, :], psc[:, tt * 128:(tt + 1) * 128], [psc])
            yield
        for tt in range(2):
            S_ = sc[tt]
            for j in range(16):
                c.op("dve", lambda: nc.vector.max(out=top[:, j, 0:8], in_=S_[:, j, :]), [S_], [top])
                c.op("dve", lambda: nc.vector.max_index(out=itu[:, j, 0:8], in_max=top[:, j, 0:8], in_values=S_[:, j, :]),
                     [S_, top], [itu])
                c.op("dve", lambda: nc.vector.match_replace(out=scw[:], in_to_replace=top[:, j, 0:8], in_values=S_[:, j, :],
                                                            imm_value=-1e30), [S_, top], [scw])
                c.op("dve", lambda: nc.vector.max(out=top[:, j, 8:16], in_=scw[:]), [scw], [top])
                c.op("dve", lambda: nc.vector.max_index(out=itu[:, j, 8:16], in_max=top[:, j, 8:16], in_values=scw[:]),
                     [scw, top], [itu])
                yield
            k.cp("dve", itf, itf[:], itu[:], [itu])
            topv = top[:].rearrange("p (h two) a -> p h two a", two=2)
            itfv = itf[:].rearrange("p (h two) a -> p h two a", two=2)
            candv = cand[:].rearrange("p h (a b) -> p h a b", a=16)
            efulv = eful[:].rearrange("p h (a b) -> p h a b", a=16)
            k.tt("dve", cand, candv, topv[:, :, 0, :].unsqueeze(3).to_broadcast([128, 8, 16, 16]),
                 topv[:, :, 1, :].unsqueeze(2).to_broadcast([128, 8, 16, 16]), ALU.add, [top])
            k.ts("dve", itf, itfv[:, :, 0, :], itfv[:, :, 0, :], 128.0, None, ALU.mult, None, [itf])
            k.tt("dve", eful, efulv, itfv[:, :, 0, :].unsqueeze(3).to_broadcast([128, 8, 16, 16]),
                 itfv[:, :, 1, :].unsqueeze(2).to_broadcast([128, 8, 16, 16]), ALU.add, [itf])
            for h in range(8):
                c.op("dve", lambda: nc.vector.max(out=best[:, h, 0:8], in_=cand[:, h, :]), [cand], [best])
                c.op("dve", lambda: nc.vector.match_replace(out=candw[:], in_to_replace=best[:, h, 0:8],
                                                            in_values=cand[:, h, :], imm_value=-1e30), [cand, best], [candw])
                c.op("dve", lambda: nc.vector.max(out=best[:, h, 8:16], in_=candw[:]), [candw], [best])
                yield
            for h in range(8):
                for kq in range(16):
                    hk = h * 16 + kq
                    k.stt(junk, junk[:], cand[:, h, :], best[:, h, kq:kq + 1], eful[:, h, :], ALU.is_equal, ALU.mult,
                          [cand, best, eful], accum=eid[:, hk:hk + 1], extra_w=[eid])
                    if kq % 8 == 7:
                        yield
            k.cp("dve", eii, eii[:], eid[:], [eid])
            k.ts("dve", ei1, ei1[:], eii[:], 7, None, ALU.arith_shift_right, None, [eii])
            k.ts("dve", ei2, ei2[:], eii[:], 127, None, ALU.bitwise_and, None, [eii])
            k.cp("dve", a1, a1[:], ei1[:], [ei1])
            k.cp("dve", a2, a2[:], ei2[:], [ei2])
            k.tt("dve", gat, gat[:], best[:], best[:, :, 0:1].to_broadcast([128, 8, 16]), ALU.subtract, [best])
            k.act(gat, gat[:], gat[:], AF.Exp, [gat])
            c.op("dve", lambda: nc.vector.tensor_reduce(out=gsum[:], in_=gat[:], axis=AX.X, op=ALU.add), [gat], [gsum])
            k.recip(grs, grs[:], gsum[:], [gsum])
            k.tt("dve", gat, gat[:], gat[:], grs[:].unsqueeze(2).to_broadcast([128, 8, 16]), ALU.mult, [gat, grs])
            for src_t, src, dst in [(a1, a1[:], a1T), (a2, a2[:], a2T), (gat, gat[:].rearrange("p h k -> p (h k)"), gT)]:
                ptr = nps()
                k.tr(ptr, ptr[:, 0:128], src, idf_t[:], [src_t, idf_t])
                k.cp("act", dst, dst[:, tt * 128:(tt + 1) * 128], ptr[:, 0:128], [ptr])
        yield

    for _ in prep(0):
        pass
    for blk in range(NB):
        t0 = blk * TB
        h1 = h1s[blk % 2]; hnT = hnTs[blk % 2]
        for tg in range(TB // 4):
            pw = nps()
            for tq in range(4):
                t = tg * 4 + tq
                A2_ = A2[t % 2]; A1_ = A1[t % 2]
                k.ts("dve", A2_, A2_[:], iota_t[:], a2T[:, t:t + 1], None, ALU.is_equal, None, [iota_t, a2T])
                k.ts("dve", A1_, A1_[:], iota_t[:], a1T[:, t:t + 1], gT[:, t:t + 1], ALU.is_equal, ALU.mult,
                     [iota_t, a1T, gT])
                k.mm(pw, pw[:, tq * 128:(tq + 1) * 128], A2_[:], A1_[:], True, True, [A2_, A1_])
            k.cp("act", WT, WT[:, :, tg * 4:tg * 4 + 4], pw[:, :].rearrange("p (t i) -> p i t", t=4), [pw])
        gnext = prep(blk + 1) if blk + 1 < NB else None
        def emit_A(i1):
            U = uc[i1 % 3]; V = vc[i1 % 3]
            k.load("sp", U, U[:].rearrange("p a b -> p (a b)"), tabt["ub"][i1], ub[i1, :, :])
            k.load("sp", V, V[:], tabt["vbd"][i1], vbd[i1, :, :])
            pa = PA[i1 % 2]
            for kk in range(8):
                k.mm(pa, pa[:, 0:TB], U[:, kk, :], hnT[:, kk, :], kk == 0, kk == 7, [U, hnT])
            return pa, V

        cur = emit_A(0)
        for i1 in range(NI1):
            nxt = emit_A(i1 + 1) if i1 + 1 < NI1 else None
            if gnext is not None:
                next(gnext, None)
            pa, V = cur
            G_ = ge[i1 % 2]; Gb = Gt[i1 % 2]
            k.act(G_, G_[:], pa[:, 0:TB], AF.Gelu, [pa])
            k.tt("dve", Gb, Gb[:], G_[:], WT[:, i1, :], ALU.mult, [G_, WT])
            for tt in range(2):
                for half in range(2):
                    po = PO[tt * 2 + half]
                    k.mm(po, po[:, :], Gb[:, tt * 128:(tt + 1) * 128], V[:, half * 512:(half + 1) * 512], i1 == 0,
                         i1 == NI1 - 1, [Gb, V])
            cur = nxt
        if gnext is not None:
            for _ in gnext:
                pass
        for tt in range(2):
            for half in range(2):
                po = PO[tt * 2 + half]
                k.tt("dve", h1, h1[:, tt, half * 512:(half + 1) * 512], po[:, :], h1[:, tt, half * 512:(half + 1) * 512],
                     ALU.add, [po, h1])
        for tt in range(2):
            rmsnorm_to_T(h1, h1[:, tt, :], 1, tt, hnT)
        for tt in range(2):
            r0 = t0 + tt * 128
            k.load("sp", pt_in, pt_in[:], D["pin"], pin[r0:r0 + 128, :])
            k.cp("dve", ptb, ptb[:], pt_in[:], [pt_in])
            pp = nps()
            ppv = pp[:].bitcast(BF16)
            for kk in range(2):
                k.tr(pp, ppv[:, kk * 128:(kk + 1) * 128], ptb[:, kk * 128:(kk + 1) * 128], idb_t[:], [ptb, idb_t])
            k.cp("act", pT, pT[:].rearrange("p a b -> p (a b)"), ppv[:, 0:256], [pp])
            for half in range(2):
                W = wA[wa_n[0] % 2]; wa_n[0] += 1
                k.load("pool", W, W[:], D["wgate"], wgate[half, :, :, :])
                pg = nps()
                for kk in range(8):
                    k.mm(pg, pg[:, :], hnT[:, kk, tt * 128:(tt + 1) * 128], W[:, kk, :], kk == 0, kk == 7, [hnT, W])
                k.act(sig, sig[:], pg[:, :], AF.Sigmoid, [pg])
                pe_ = nps()
                for kk in range(2):
                    k.mm(pe_, pe_[:, :], pT[:, kk, :], wproj_t[:, kk, half * 512:(half + 1) * 512], kk == 0, kk == 1,
                         [pT, wproj_t])
                k.tt("dve", tmpe, tmpe[:], pe_[:, :], sig[:], ALU.mult, [pe_, sig])
                k.tt("dve", h1, h1[:, tt, half * 512:(half + 1) * 512], tmpe[:], h1[:, tt, half * 512:(half + 1) * 512],
                     ALU.add, [tmpe, h1])
            if write_y:
                c.dma("sp", D["y"], h1, (lambda r0=r0, tt=tt: nc.sync.dma_start(out=y[r0:r0 + 128, :], in_=h1[:, tt, :])), par=True)
            if write_yf:
                k.act(sq, sq[:], h1[:, tt, :], AF.Square, [h1], accum=ssq[:], extra_w=[ssq])
                k.rstd_from_ssq(ssq, tmp1, rstd, 1024.0, None)
                k.stt(mx, mx[:], h1[:, tt, :], rstd[:], wn_t[:, 2, :], ALU.mult, ALU.mult, [h1, rstd, wn_t])
                c.dma("sp", D["yf"], mx, (lambda r0=r0: nc.sync.dma_start(out=yf[r0:r0 + 128, :], in_=mx[:])), par=True)
    return env.end([D["y"], D["yf"]])


def consts_B(w_out, w_gate, w_q, w_proj, wn_ffn, wn_ple, wn_fin, sub_keys, u_tab, v_tab):
    def halves(w):
        return np.ascontiguousarray(w.reshape(8, 128, 2, 512).transpose(2, 1, 0, 3))
    wq = np.ascontiguousarray(w_q.reshape(8, 128, 16, 128).transpose(2, 1, 0, 3))
    wproj = np.ascontiguousarray(w_proj.reshape(2, 128, 1024).transpose(1, 0, 2))
    wnorm = np.ascontiguousarray(np.tile(np.stack([wn_ffn, wn_ple, wn_fin], axis=0)[None], (128, 1, 1))).astype(np.float32)
    skT = np.ascontiguousarray(sub_keys.transpose(2, 0, 1))
    ul = np.ascontiguousarray(u_tab.reshape(128, 128, 8, 128).transpose(0, 3, 2, 1)).reshape(128, 128, 1024)
    vl = np.ascontiguousarray(v_tab.reshape(128, 128, 1024))
    return dict(wout=halves(w_out), wgate=halves(w_gate), wq=wq, wproj=wproj, wnorm=wnorm, skT=skT, ul=ul, vl=vl,
                identf=np.eye(128, dtype=np.float32),
                iota=np.ascontiguousarray(np.tile(np.arange(128, dtype=np.float32)[None, :], (128, 1))))


A1_G1 = F32
A1_G2 = BF16


def build_A1(S, env=None):
    env = env or Env()
    nc, c, k = env.nc, env.c, env.k
    env.begin()
    NCH = S // 128
    NH = 4
    x = env.din("x", [S, 1024])
    wn = env.din("wn", [128, 1024])
    w_tok = env.din("w_tok", [1024, 520])
    w_feat = env.din("w_feat", [1024, 1536])
    convw = env.din("convw", [128, 48])
    adt = env.din("adt", [128, 8])
    nw = env.din("nw", [128, 128])
    identf = env.din("identf", [128, 128])
    utri = env.din("utri", [128, 128])
    masks = env.din("masks", [128, 384])
    y = env.dout("y", [S, 512])
    D = {n: c.dram(n, a) for n, a in dict(x=x, wn=wn, w_tok=w_tok, w_feat=w_feat, convw=convw, adt=adt, nw=nw,
                                            identf=identf, utri=utri, masks=masks, y=y).items()}
    wn_t = c.sb("wn_t", [128, 1024])
    wtok_t = c.sb("wtok_t", [128, 8, 520], BF16)
    wfeat_t = c.sb("wfeat_t", [128, 8, 1536], BF16)
    convw_t = c.sb("convw_t", [128, 48])
    adt_t = c.sb("adt_t", [128, 8])
    nw_t = c.sb("nw_t", [128, 128])
    idf_t = c.sb("idf_t", [128, 128])
    idb_t = c.sb("idb_t", [128, 128], BF16)
    utri_t = c.sb("utri_t", [128, 128])
    masks_t = c.sb("masks_t", [128, 384])
    ones_t = c.sb("ones_t", [128, 128])
    negA = c.sb("negA", [128, 4])
    k.load("sp", wn_t, wn_t[:], D["wn"], wn[:, :])
    for kk in range(8):
        k.load("pool", wtok_t, wtok_t[:, kk, :], D["w_tok"], w_tok[kk * 128:(kk + 1) * 128, :], par=True)
        k.load("pool", wfeat_t, wfeat_t[:, kk, :], D["w_feat"], w_feat[kk * 128:(kk + 1) * 128, :], par=True)
    for t, d, a in [(convw_t, "convw", convw), (adt_t, "adt", adt), (nw_t, "nw", nw), (idf_t, "identf", identf),
                    (utri_t, "utri", utri), (masks_t, "masks", masks)]:
        k.load("sp", t, t[:], D[d], a[:, :])
    k.cp("dve", idb_t, idb_t[:], idf_t[:], [idf_t])
    k.memset("dve", ones_t, ones_t[:], 1.0)
    k.act(negA, negA[:], adt_t[:, 0:4], AF.Exp, [adt_t])
    k.ts("dve", negA, negA[:], negA[:], -1.0, None, ALU.mult, None, [negA])

    xt = [c.sb(f"xt{i}", [128, 1024]) for i in range(2)]
    sq = c.sb("sq", [128, 1024], BF16)
    ssq = c.sb("ssq", [128, 1]); tmp1 = c.sb("tmp1", [128, 1]); rstd = c.sb("rstd", [128, 1])
    hn = c.sb("hn", [128, 1024], BF16)
    hnT = c.sb("hnT", [128, 8, 128], BF16)
    ztok = c.sb("ztok", [128, 520])
    zc = c.sb("zc", [128, 12, 131])
    cacc = c.sb("cacc", [128, 12, 128])
    csil = c.sb("csil", [128, 12, 128])
    sq2 = c.sb("sq2", [128, 8, 128])
    rn = c.sb("rn", [128, 8, 128])
    qnT = c.sb("qnT", [128, 4, 128], BF16)
    knT = c.sb("knT", [128, 4, 128], BF16)
    knTf = c.sb("knTf", [128, 4, 128])
    ktok = c.sb("ktok", [128, 4, 128], BF16)
    vtok = c.sb("vtok", [128, 4, 128])
    sgate = c.sb("sgate", [128, 512])
    beta = c.sb("beta", [128, 4]); lnb = c.sb("lnb", [128, 4]); spx = c.sb("spx", [128, 4]); gg = c.sb("gg", [128, 4])
    Gcol = c.sb("Gcol", [128, 4]); negG = c.sb("negG", [128, 4]); eG = c.sb("eG", [128, 4]); beG = c.sb("beG", [128, 4])
    gl = c.sb("gl", [128, 4]); biasA = c.sb("biasA", [128, 4]); wdec = c.sb("wdec", [128, 4]); GBl = c.sb("GBl", [128, 4])
    glast = c.sb("glast", [128, 4])
    gbc = [c.sb(f"gbc{h}", [128, 128]) for h in range(NH)]
    lbc = [c.sb(f"lbc{h}", [128, 128]) for h in range(NH)]
    Dq = [c.sb(f"Dq{h}", [128, 128]) for h in range(NH)]
    Dm = [c.sb(f"Dm{h}", [128, 128]) for h in range(NH)]
    DA = [c.sb(f"DA{h}", [128, 128]) for h in range(NH)]
    EGB = [c.sb(f"EGB{h}", [128, 128]) for h in range(NH)]
    Pm = [[c.sb(f"Pm{h}_{i}", [128, 128], A1_G1) for i in range(7)] for h in range(NH)]
    Pa = [[c.sb(f"Pa{h}_{i}", [128, 128], A1_G1) for i in range(2)] for h in range(NH)]
    Yf = [c.sb(f"Yf{h}", [128, 128]) for h in range(NH)]
    Yb = [c.sb(f"Yb{h}", [128, 128], A1_G1) for h in range(NH)]
    t1 = [c.sb(f"t1_{h}", [128, 128]) for h in range(NH)]
    attT = [c.sb(f"attT{h}", [128, 128], A1_G1) for h in range(NH)]
    qdT = [c.sb(f"qdT{h}", [128, 128], A1_G2) for h in range(NH)]
    kdec = [c.sb(f"kdec{h}", [128, 128], A1_G1) for h in range(NH)]
    Sst = [c.sb(f"Sst{h}", [128, 128]) for h in range(NH)]
    Sb = [c.sb(f"Sb{h}", [128, 128], A1_G2) for h in range(NH)]
    osq = c.sb("osq", [128, 128]); oss = c.sb("oss", [128, 1]); otmp = c.sb("otmp", [128, 1]); orstd = c.sb("orstd", [128, 1])
    on = c.sb("on", [128, 128])
    yt = [c.sb(f"yt{i}", [128, 512]) for i in range(2)]
    ydst = [c.dram(f"ydst{i}", None) for i in range(2)]
    P = env.P

    def nps():
        c._psn += 1
        return P[c._psn % 8]

    for h in range(NH):
        k.memset("dve", Sst[h], Sst[h][:], 0.0)
        k.memset("dve", Sb[h], Sb[h][:], 0.0)
    k.memset("dve", zc, zc[:], 0.0)

    for ch in range(NCH):
        X = xt[ch % 2]; Y = yt[ch % 2]
        r0 = ch * 128
        k.load("sp", X, X[:], D["x"], x[r0:r0 + 128, :])
        k.act(sq, sq[:], X[:], AF.Square, [X], accum=ssq[:], extra_w=[ssq])
        k.rstd_from_ssq(ssq, tmp1, rstd, 1024.0, None)
        k.stt(hn, hn[:], X[:], rstd[:], wn_t[:], ALU.mult, ALU.mult, [X, rstd, wn_t])
        pt = nps()
        ptb = pt[:].bitcast(BF16)
        for kk in range(8):
            k.tr(pt, ptb[:, kk * 128:(kk + 1) * 128], hn[:, kk * 128:(kk + 1) * 128], idb_t[:], [hn, idb_t])
        k.cp("act", hnT, hnT[:].rearrange("p a b -> p (a b)"), ptb[:, 0:1024], [pt])
        for (c0, n) in [(0, 512), (512, 8)]:
            pz = nps()
            for kk in range(8):
                k.mm(pz, pz[:, 0:n], hnT[:, kk, :], wtok_t[:, kk, c0:c0 + n], kk == 0, kk == 7, [hnT, wtok_t])
            k.cp("act", ztok, ztok[:, c0:c0 + n], pz[:, 0:n], [pz])
        for g3 in range(3):
            pf = nps()
            for j in range(4):
                jj = g3 * 4 + j
                for kk in range(8):
                    k.mm(pf, pf[:, j * 128:(j + 1) * 128], wfeat_t[:, kk, jj * 128:(jj + 1) * 128], hnT[:, kk, :],
                         kk == 0, kk == 7, [hnT, wfeat_t])
            k.cp("act" if g3 != 1 else "dve", zc, zc[:, g3 * 4:(g3 + 1) * 4, 3:131],
                 pf[:].rearrange("p (a b) -> p a b", a=4), [pf])
        for j in range(12):
            k.ts("dve", cacc, cacc[:, j, :], zc[:, j, 0:128], convw_t[:, j * 4:j * 4 + 1], None, ALU.mult, None,
                 [zc, convw_t])
            for tp in range(1, 4):
                k.stt(cacc, cacc[:, j, :], zc[:, j, tp:tp + 128], convw_t[:, j * 4 + tp:j * 4 + tp + 1], cacc[:, j, :],
                      ALU.mult, ALU.add, [zc, convw_t, cacc])
        k.cp("dve", zc, zc[:, :, 0:3], zc[:, :, 128:131], [zc])
        k.act(csil, csil[:], cacc[:], AF.Silu, [cacc])
        k.tt("pool", sq2, sq2[:], csil[:, 0:8, :], csil[:, 0:8, :], ALU.mult, [csil])
        for hf in range(2):
            pn = nps()
            k.mm(pn, pn[:, :], ones_t[:], sq2[:, hf * 4:(hf + 1) * 4, :].rearrange("p a b -> p (a b)"), True, True,
                 [ones_t, sq2])
            k.ts("dve", rn, rn[:, hf * 4:(hf + 1) * 4, :].rearrange("p a b -> p (a b)"), pn[:, :], EPS, None, ALU.add, None,
                 [pn])
        k.act(rn, rn[:], rn[:], AF.Sqrt, [rn])
        k.recip(rn, rn[:], rn[:], [rn])
        k.stt(qnT, qnT[:], csil[:, 0:4, :], 128.0 ** -0.5, rn[:, 0:4, :], ALU.mult, ALU.mult, [csil, rn])
        k.tt("dve", knTf, knTf[:], csil[:, 4:8, :], rn[:, 4:8, :], ALU.mult, [csil, rn])
        k.cp("pool", knT, knT[:], knTf[:], [knTf])
        pk = nps()
        pkb = pk[:].bitcast(BF16)
        for h in range(NH):
            k.tr(pk, pkb[:, h * 128:(h + 1) * 128], knT[:, h, :], idb_t[:], [knT, idb_t])
        k.cp("act", ktok, ktok[:].rearrange("p a b -> p (a b)"), pkb[:, 0:512], [pk])
        pv = nps()
        for h in range(NH):
            k.tr(pv, pv[:, h * 128:(h + 1) * 128], csil[:, 8 + h, :], idf_t[:], [csil, idf_t])
        k.cp("act", vtok, vtok[:].rearrange("p a b -> p (a b)"), pv[:, :], [pv])
        k.act(sgate, sgate[:], ztok[:, 0:512], AF.Silu, [ztok])
        k.act(lnb, lnb[:], ztok[:, 512:516], AF.Exp, [ztok], scale=-1.0)
        k.ts("dve", beta, beta[:], lnb[:], 1.0, None, ALU.add, None, [lnb])
        k.act(lnb, lnb[:], beta[:], AF.Ln, [beta])
        k.ts("dve", lnb, lnb[:], lnb[:], -1.0, None, ALU.mult, None, [lnb])
        k.recip(beta, beta[:], beta[:], [beta])
        k.tt("dve", spx, spx[:], ztok[:, 516:520], adt_t[:, 4:8], ALU.add, [ztok, adt_t])
        k.act(spx, spx[:], spx[:], AF.Exp, [spx])
        k.act(spx, spx[:], spx[:], AF.Ln, [spx], bias=1.0)
        k.tt("dve", gg, gg[:], spx[:], negA[:], ALU.mult, [spx, negA])
        pg = nps()
        k.mm(pg, pg[:, 0:4], utri_t[:], gg[:], True, True, [utri_t, gg])
        k.cp("dve", Gcol, Gcol[:], pg[:, 0:4], [pg])
        k.ts("dve", negG, negG[:], Gcol[:], -1.0, None, ALU.mult, None, [Gcol])
        k.act(eG, eG[:], Gcol[:], AF.Exp, [Gcol])
        k.tt("dve", beG, beG[:], eG[:], beta[:], ALU.mult, [eG, beta])
        k.tt("dve", biasA, biasA[:], Gcol[:], lnb[:], ALU.add, [Gcol, lnb])
        pGB = []
        pB2 = []
        for h in range(NH):
            k.cp("dve", gbc[h], gbc[h][:], gg[:, h:h + 1].to_broadcast([128, 128]), [gg])
            k.cp("pool", lbc[h], lbc[h][:], lnb[:, h:h + 1].to_broadcast([128, 128]), [lnb])
        for h in range(NH):
            pb = nps()
            k.mm(pb, pb[:, 0:128], gbc[h][:], utri_t[:], True, True, [gbc[h], utri_t])
            k.mm(pb, pb[:, 128:256], gbc[h][:], utri_t[:], True, False, [gbc[h], utri_t])
            k.mm(pb, pb[:, 128:256], idf_t[:], masks_t[:, 0:128], False, True, [idf_t, masks_t])
            k.mm(pb, pb[:, 256:384], gbc[h][:], utri_t[:], True, False, [gbc[h], utri_t])
            k.mm(pb, pb[:, 256:384], lbc[h][:], idf_t[:], False, False, [lbc[h], idf_t])
            k.mm(pb, pb[:, 256:384], idf_t[:], masks_t[:, 128:256], False, True, [idf_t, masks_t])
            k.mm(pb, pb[:, 384:512], gbc[h][:], utri_t[:], True, False, [gbc[h], utri_t])
            k.mm(pb, pb[:, 384:512], idf_t[:], masks_t[:, 256:384], False, True, [idf_t, masks_t])
            k.act(EGB[h], EGB[h][:], pb[:, 0:128], AF.Exp, [pb])
            k.act(Dq[h], Dq[h][:], pb[:, 128:256], AF.Exp, [pb, negG], bias=negG[:, h:h + 1])
            k.act(Dm[h], Dm[h][:], pb[:, 256:384], AF.Exp, [pb, negG], bias=negG[:, h:h + 1])
            k.act(DA[h], DA[h][:], pb[:, 384:512], AF.Exp, [pb, biasA], bias=biasA[:, h:h + 1], scale=-1.0)
            k.cp("act", GBl, GBl[:, h:h + 1], pb[:, 127:128], [pb])
        k.act(glast, glast[:], GBl[:], AF.Exp, [GBl])
        for h in range(NH):
            k.act(wdec, wdec[:, h:h + 1], negG[:, h:h + 1], AF.Exp, [negG, GBl], bias=GBl[:, h:h + 1])
        for h in range(NH):
            pkk = nps()
            k.mm(pkk, pkk[:, 0:128], knT[:, h, :], knT[:, h, :], True, True, [knT])
            k.mm(pkk, pkk[:, 128:256], knT[:, h, :], qnT[:, h, :], True, True, [knT, qnT])
            k.tt("dve", Pm[h][0], Pm[h][0][:], pkk[:, 0:128], Dm[h][:], ALU.mult, [pkk, Dm[h]])
            k.tt("dve", Pa[h][0], Pa[h][0][:], pkk[:, 0:128], DA[h][:], ALU.mult, [pkk, DA[h]])
            k.tt("dve", attT[h], attT[h][:], pkk[:, 128:256], Dq[h][:], ALU.mult, [pkk, Dq[h]])
            k.tt("pool", qdT[h], qdT[h][:], qnT[:, h, :], EGB[h][:], ALU.mult, [qnT, EGB[h]])
            k.ts("dve", kdec[h], kdec[h][:], ktok[:, h, :], wdec[:, h:h + 1], None, ALU.mult, None, [ktok, wdec])
            pass
        for st in range(6):
            cur = st % 2
            nxt = 1 - cur
            for h in range(NH):
                pp = nps()
                k.mm(pp, pp[:, 0:128], Pa[h][cur][:], Pm[h][st][:], True, True, [Pa[h][cur], Pm[h][st]])
                k.mm(pp, pp[:, 128:256], Pm[h][st][:], Pa[h][cur][:], True, True, [Pa[h][cur], Pm[h][st]])
                k.cp("act", Pm[h][st + 1], Pm[h][st + 1][:], pp[:, 0:128], [pp])
                k.cp("dve", Pa[h][nxt], Pa[h][nxt][:], pp[:, 128:256], [pp])
        for h in range(NH):
            pks = nps()
            k.mm(pks, pks[:, 0:128], knTf[:, h, :], Sst[h][:], True, True, [knTf, Sst[h]])
            k.stt(t1[h], t1[h][:], pks[:, 0:128], eG[:, h:h + 1], vtok[:, h, :], ALU.mult, ALU.subtract, [pks, eG, vtok])
            k.ts("dve", Yf[h], Yf[h][:], t1[h][:], beta[:, h:h + 1], -1.0, ALU.mult, ALU.mult, [t1[h], beta])
            k.cp("act", Yb[h], Yb[h][:], Yf[h][:], [Yf[h]])
        for st in range(7):
            for h in range(NH):
                py = nps()
                k.mm(py, py[:, 0:128], Pm[h][st][:], Yb[h][:], True, True, [Pm[h][st], Yb[h]])
                k.tt("dve", Yf[h], Yf[h][:], Yf[h][:], py[:, 0:128], ALU.subtract if st == 0 else ALU.add, [Yf[h], py])
                k.cp("act", Yb[h], Yb[h][:], Yf[h][:], [Yf[h]])
        for h in range(NH):
            po = nps()
            k.mm(po, po[:, 0:128], attT[h][:], Yb[h][:], True, False, [attT[h], Yb[h]])
            k.mm(po, po[:, 0:128], qdT[h][:], Sb[h][:], False, True, [qdT[h], Sb[h]])
            pu = nps()
            k.mm(pu, pu[:, 0:128], kdec[h][:], Yb[h][:], True, True, [kdec[h], Yb[h]])
            k.stt(Sst[h], Sst[h][:], Sst[h][:], glast[:, h:h + 1], pu[:, 0:128], ALU.mult, ALU.add, [Sst[h], glast, pu])
            k.cp("act", Sb[h], Sb[h][:], Sst[h][:], [Sst[h]])
            k.act(osq, osq[:], po[:, 0:128], AF.Square, [po], accum=oss[:], extra_w=[oss])
            k.rstd_from_ssq(oss, otmp, orstd, 128.0, None)
            k.stt(on, on[:], po[:, 0:128], orstd[:], nw_t[:], ALU.mult, ALU.mult, [po, orstd, nw_t])
            k.tt("dve", Y, Y[:, h * 128:(h + 1) * 128], on[:], sgate[:, h * 128:(h + 1) * 128], ALU.mult, [on, sgate])
        c.dma("sp", ydst[ch % 2], Y, (lambda r0=r0, Y=Y: nc.sync.dma_start(out=y[r0:r0 + 128, :], in_=Y[:])))
    return env.end(ydst)


def inputs_A1(x, norm_w, w_in, conv_w, a_log, dt_bias, dn_norm_w):
    B, S, _ = x.shape
    identf = np.eye(128, dtype=np.float32)
    utri = np.triu(np.ones((128, 128), dtype=np.float32))
    BIG = np.float32(1.0e5)
    jj, ii = np.meshgrid(np.arange(128), np.arange(128), indexing="ij")
    masks = np.concatenate([np.where(jj > ii, -BIG, 0.0), np.where(jj >= ii, -BIG, 0.0), np.where(ii >= jj, BIG, 0.0)],
                           axis=1).astype(np.float32)
    maps = []
    for core in range(8):
        b, g = core // 2, core % 2
        hs = [4 * g + i for i in range(4)]
        def cols(base):
            return np.concatenate([np.arange(base + h * 128, base + (h + 1) * 128) for h in hs])
        feat_cols = np.concatenate([cols(0), cols(1024), cols(2048)])
        tok_cols = np.concatenate([cols(3072), np.array([4096 + h for h in hs]), np.array([4104 + h for h in hs])])
        cw = conv_w[:, feat_cols]
        convw = np.ascontiguousarray(cw.reshape(4, 12, 128).transpose(2, 1, 0).reshape(128, 48))
        adt = np.tile(np.concatenate([a_log[hs], dt_bias[hs]])[None, :], (128, 1)).astype(np.float32)
        maps.append(dict(x=np.ascontiguousarray(x[b]), wn=np.tile(norm_w[None, :], (128, 1)).astype(np.float32),
                         w_tok=np.ascontiguousarray(w_in[:, tok_cols]), w_feat=np.ascontiguousarray(w_in[:, feat_cols]),
                         convw=convw, adt=adt, nw=np.tile(dn_norm_w[None, :], (128, 1)).astype(np.float32),
                         identf=identf, utri=utri, masks=masks))
    return maps


def gather_A1(results, B, S):
    mixed = np.empty((B, S, 1024), dtype=np.float32)
    for core in range(8):
        b, g = core // 2, core % 2
        mixed[b, :, g * 512:(g + 1) * 512] = results[core]["y"]
    return mixed


def build_fused(S):
    nc = bass.Bass("TRN2", target_bir_lowering=False)
    c = Ctx(nc)
    k = K(nc, c)
    P = [c.ps(f"P{i}") for i in range(8)]
    ein = lambda n, shp: nc.dram_tensor(n, list(shp), F32, kind="ExternalInput").ap()
    x = ein("x", [S, 1024]); p0 = ein("p0", [S, 256]); p1 = ein("p1", [S, 256])
    identf = ein("identf", [128, 128]); utri = ein("utri", [128, 128]); iota = ein("iota", [128, 128])
    cosd = ein("cos", [S, 64]); sind = ein("sin", [S, 64]); masks = ein("masks", [128, 384])
    wn0 = ein("wn0", [128, 1024]); wn1 = ein("wn1", [128, 1024]); nw = ein("nw", [128, 128])
    out = nc.dram_tensor("out", [S, 1024], F32, kind="ExternalOutput").ap()
    mixed_d = nc.dram_tensor("mixed_d", [S, 1024], F32, kind="Internal").ap()
    h0_d = nc.dram_tensor("h0_d", [S, 1024], F32, kind="Internal").ap()
    ub = nc.dram_tensor("ub", [128, 128, 1024], BF16, kind="Internal").ap()
    vbd = nc.dram_tensor("vbd", [128, 128, 1024], BF16, kind="Internal").ap()
    for g in range(2):
        build_A0(S, Env(nc, c, k, P, tag=f"a0g{g}_", over=dict(x=x, cos=cosd, sin=sind, identf=identf, utri=utri, wn=wn0,
                                                                 y=mixed_d[:, g * 512:(g + 1) * 512])))
    build_B(S, False, env=Env(nc, c, k, P, tag="b0_", over=dict(hin=x, mixed=mixed_d, pin=p0, identf=identf, iota=iota,
                                                                y=h0_d, ub=ub, vbd=vbd)), write_y=True, write_yf=False)
    for g in range(2):
        build_A1(S, Env(nc, c, k, P, tag=f"a1g{g}_", over=dict(x=h0_d, identf=identf, utri=utri, masks=masks, wn=wn1, nw=nw,
                                                                 y=mixed_d[:, g * 512:(g + 1) * 512])))
    build_B(S, True, env=Env(nc, c, k, P, tag="b1_", over=dict(hin=h0_d, mixed=mixed_d, pin=p1, identf=identf, iota=iota,
                                                               yf=out, ub=ub, vbd=vbd)), write_y=False, write_yf=True)
    return nc


def inputs_fused(x, p, norm_mix_w, norm_ffn_w, ab_w_in, ab_conv_w, ab_b_i, ab_b_f, ab_gn_ret, ab_gn_mlstm, ab_w_out,
                 dn_w_in, dn_conv_w, dn_a_log, dn_dt_bias, dn_norm_w, dn_w_out, peer_w_q, peer_sub_keys, peer_u, peer_v,
                 ple_w_proj, ple_w_gate, ple_norm_w, final_norm_w, n_cores=8):
    B, S, _ = x.shape
    mA0 = inputs_A0(x, norm_mix_w[0], ab_w_in[0], ab_conv_w[0], ab_b_i[0], ab_b_f[0], ab_gn_ret[0], ab_gn_mlstm[0])
    mA1 = inputs_A1(x, norm_mix_w[1], dn_w_in[0], dn_conv_w[0], dn_a_log[0], dn_dt_bias[0], dn_norm_w[0])
    perm = np.concatenate([np.concatenate([np.arange(2 * g * 128, (2 * g + 2) * 128),
                                           512 + np.arange(2 * g * 128, (2 * g + 2) * 128)]) for g in range(2)])
    cB = [consts_B(np.ascontiguousarray(ab_w_out[0][perm]), ple_w_gate[0], peer_w_q[0], ple_w_proj[0], norm_ffn_w[0],
                   ple_norm_w[0], final_norm_w, peer_sub_keys[0], peer_u[0], peer_v[0]),
          consts_B(dn_w_out[0], ple_w_gate[1], peer_w_q[1], ple_w_proj[1], norm_ffn_w[1], ple_norm_w[1], final_norm_w,
                   peer_sub_keys[1], peer_u[1], peer_v[1])]
    shared = dict(identf=mA0[0]["identf"], utri=mA0[0]["utri"], iota=cB[0]["iota"], cos=mA0[0]["cos"], sin=mA0[0]["sin"],
                  masks=mA1[0]["masks"], wn0=mA0[0]["wn"], wn1=mA1[0]["wn"], nw=mA1[0]["nw"])
    maps = []
    for core in range(n_cores):
        b = core % B
        m = dict(shared, x=np.ascontiguousarray(x[b]), p0=np.ascontiguousarray(p[0][b]), p1=np.ascontiguousarray(p[1][b]))
        for g in range(2):
            for kx in ["w_tok", "w_feat", "convw", "bif", "gnw", "dmT", "xir", "zg"]:
                m[f"a0g{g}_{kx}"] = mA0[b * 2 + g][kx]
            for kx in ["w_tok", "w_feat", "convw", "adt"]:
                m[f"a1g{g}_{kx}"] = mA1[b * 2 + g][kx]
        for L in range(2):
            for kx in ["wout", "wgate", "wq", "wproj", "wnorm", "skT", "ul", "vl"]:
                m[f"b{L}_{kx}"] = cB[L][kx]
        maps.append(m)
    return maps


def kernel(x, p, norm_mix_w, norm_ffn_w, ab_w_in, ab_conv_w, ab_b_i, ab_b_f, ab_gn_ret, ab_gn_mlstm, ab_w_out,
           dn_w_in, dn_conv_w, dn_a_log, dn_dt_bias, dn_norm_w, dn_w_out, peer_w_q, peer_sub_keys, peer_u, peer_v,
           ple_w_proj, ple_w_gate, ple_norm_w, final_norm_w):
    f = lambda a: np.ascontiguousarray(np.asarray(a, dtype=np.float32))
    args = [f(a) for a in (x, p, norm_mix_w, norm_ffn_w, ab_w_in, ab_conv_w, ab_b_i, ab_b_f, ab_gn_ret, ab_gn_mlstm,
                           ab_w_out, dn_w_in, dn_conv_w, dn_a_log, dn_dt_bias, dn_norm_w, dn_w_out, peer_w_q,
                           peer_sub_keys, peer_u, peer_v, ple_w_proj, ple_w_gate, ple_norm_w, final_norm_w)]
    B, S, _ = args[0].shape
    nc = build_fused(S)
    maps = inputs_fused(*args)
    res = run_bass_kernel_spmd(nc, maps, core_ids=list(range(8))).results
    return np.stack([res[b]["out"] for b in range(B)], axis=0).astype(np.float32)
```

```python
import contextlib
import numpy as np
import concourse.bass as bass
import concourse.mybir as mybir
from concourse.bass_utils import run_bass_kernel_spmd

F32 = mybir.dt.float32
BF16 = mybir.dt.bfloat16
I32 = mybir.dt.int32
U32 = mybir.dt.uint32
AF = mybir.ActivationFunctionType
ALU = mybir.AluOpType
AX = mybir.AxisListType
EPS = 1e-6


class Tile:
    def __init__(self, ctx, name, handle):
        self.ctx = ctx
        self.name = name
        self.h = handle
        self.last_w = []
        self.reads = []
        self.dsem = None
        self.dcount = 0
        self.excl = False

    def __getitem__(self, idx):
        return self.h[idx]


def _compact(lst):
    best = {}
    keep = {}
    for (s, v) in lst:
        k = id(s)
        if k not in best or best[k] < v:
            best[k] = v
            keep[k] = s
    return [(keep[k], best[k]) for k in best]


class Eng:
    def __init__(self, ctx, name, eng):
        self.name = name
        self.eng = eng
        self.sem = ctx.new_sem(name)
        self.count = 0
        self.waited = {}
        self.n_ins = 0
        self.n_wait = 0

    def wait(self, sem, val):
        key = id(sem)
        if self.waited.get(key, 0) >= val:
            return
        self.eng.wait_ge(sem, val)
        self.n_wait += 1
        self.waited[key] = val


class Ctx:
    def __init__(self, nc):
        self.nc = nc
        self.es = contextlib.ExitStack()
        self.nsem = 0
        self.E = {
            "pe": Eng(self, "pe", nc.tensor),
            "act": Eng(self, "act", nc.scalar),
            "dve": Eng(self, "dve", nc.vector),
            "pool": Eng(self, "pool", nc.gpsimd),
            "sp": Eng(self, "sp", nc.sync),
        }
        self._psn = 0
        self.pes = None
        self.semcache = {}
        self.semcount = {}
        self.nalloc = 0
        self.prefix = ""

    def new_sem(self, name):
        self.nsem += 1
        return self.es.enter_context(self.nc.semaphore(f"{name}_{self.nsem}"))

    def dma_sem(self, tile):
        key = "d_" + tile.name
        if key not in self.semcache:
            self.semcache[key] = self.new_sem(key)
            self.semcount[key] = 0
        tile.dsem = self.semcache[key]
        tile.dcount = self.semcount[key]
        tile.dkey = key

    def begin_phase(self):
        self.pes = contextlib.ExitStack()

    def barrier(self):
        pend = [(e.sem, e.count) for e in self.E.values() if e.count > 0]
        pend += [(self.semcache[kx], self.semcount[kx]) for kx in self.semcache if self.semcount[kx] > 0]
        for e in self.E.values():
            for (sm, v) in pend:
                e.wait(sm, v)

    def end_phase(self):
        self.barrier()
        self.pes.close()
        self.pes = None

    def sb(self, name, shape, dtype=F32):
        self.nalloc += 1
        st = self.pes if self.pes is not None else self.es
        h = st.enter_context(self.nc.sbuf_tensor(f"{name}_{self.nalloc}", list(shape), dtype))
        return Tile(self, self.prefix + name, h)

    def ps(self, name, shape=(128, 512), dtype=F32):
        h = self.es.enter_context(self.nc.psum_tensor(name, list(shape), dtype))
        t = Tile(self, name, h)
        t.excl = True
        return t

    def dram(self, name, ap):
        return Tile(self, self.prefix + name, ap)

    def op(self, en, fn, reads=(), writes=(), pe_acc=False):
        e = self.E[en]
        for r in reads:
            for (s, v) in r.last_w:
                e.wait(s, v)
            if r.excl:
                for (s, v) in r.reads:
                    if s is not e.sem:
                        e.wait(s, v)
        for w in writes:
            for (s, v) in w.last_w:
                if pe_acc and s is e.sem:
                    continue
                e.wait(s, v)
            for (s, v) in w.reads:
                e.wait(s, v)
        ins = fn()
        e.count += 1
        e.n_ins += 1
        ins.then_inc(e.sem, 1)
        tok = (e.sem, e.count)
        for w in writes:
            w.last_w = [tok]
            w.reads = []
        for r in reads:
            if r in writes:
                continue
            r.reads.append(tok)
            if len(r.reads) > 24:
                r.reads = _compact(r.reads)
        return ins

    def dma(self, en, out_t, in_t, fn, extra_reads=(), par=False):
        e = self.E[en]
        if out_t.dsem is None:
            self.dma_sem(out_t)
        for t in (in_t,) + tuple(extra_reads):
            for (s, v) in t.last_w:
                e.wait(s, v)
        for (s, v) in out_t.last_w:
            if par and s is out_t.dsem:
                continue
            e.wait(s, v)
        for (s, v) in out_t.reads:
            e.wait(s, v)
        ins = fn()
        e.n_ins += 1
        out_t.dcount += 16
        self.semcount[out_t.dkey] = out_t.dcount
        ins.then_inc(out_t.dsem, 16)
        tok = (out_t.dsem, out_t.dcount)
        out_t.last_w = [tok]
        out_t.reads = []
        for t in (in_t,) + tuple(extra_reads):
            t.reads.append(tok)
            if len(t.reads) > 24:
                t.reads = _compact(t.reads)
        return ins

    def finish(self, out_tiles):
        e = self.E["sp"]
        for t in out_tiles:
            for (s, v) in t.last_w:
                e.wait(s, v)

    def stats(self):
        return {k: (v.n_ins, v.n_wait) for k, v in self.E.items()}


class K:
    def __init__(self, nc, c):
        self.nc = nc
        self.c = c

    def mm(self, ps, out, lhsT, rhs, start, stop, reads):
        nc = self.nc
        return self.c.op("pe", lambda: nc.tensor.matmul(out, lhsT=lhsT, rhs=rhs, start=start, stop=stop),
                         reads, [ps], pe_acc=True)

    def tr(self, ps, out, in_, ident, reads):
        nc = self.nc
        return self.c.op("pe", lambda: nc.tensor.transpose(out, in_, ident), reads, [ps], pe_acc=True)

    def act(self, wt, out, in_, func, reads, bias=None, scale=None, accum=None, extra_w=()):
        nc = self.nc
        kw = {}
        if bias is not None:
            kw["bias"] = bias
        if scale is not None:
            kw["scale"] = scale
        if accum is not None:
            kw["accum_out"] = accum
        return self.c.op("act", lambda: nc.scalar.activation(out=out, in_=in_, func=func, **kw),
                         reads, [wt] + list(extra_w))

    def tt(self, en, wt, out, in0, in1, op, reads):
        eng = self.nc.vector if en == "dve" else self.nc.gpsimd
        return self.c.op(en, lambda: eng.tensor_tensor(out=out, in0=in0, in1=in1, op=op), reads, [wt])

    def ts(self, en, wt, out, in0, s1, s2, op0, op1, reads, accum=None, extra_w=()):
        eng = self.nc.vector if en == "dve" else self.nc.gpsimd
        kw = {}
        if accum is not None:
            kw["accum_out"] = accum
        if op1 is None:
            return self.c.op(en, lambda: eng.tensor_scalar(out=out, in0=in0, scalar1=s1, scalar2=None, op0=op0, **kw),
                             reads, [wt] + list(extra_w))
        return self.c.op(en, lambda: eng.tensor_scalar(out=out, in0=in0, scalar1=s1, scalar2=s2, op0=op0, op1=op1, **kw),
                         reads, [wt] + list(extra_w))

    def stt(self, wt, out, in0, scalar, in1, op0, op1, reads, accum=None, extra_w=()):
        nc = self.nc
        kw = {}
        if accum is not None:
            kw["accum_out"] = accum
        return self.c.op("dve", lambda: nc.vector.scalar_tensor_tensor(out=out, in0=in0, scalar=scalar, in1=in1,
                                                                        op0=op0, op1=op1, **kw),
                         reads, [wt] + list(extra_w))

    def cp(self, en, wt, out, in_, reads):
        nc = self.nc
        if en == "act":
            return self.c.op("act", lambda: nc.scalar.activation(out=out, in_=in_, func=AF.Copy), reads, [wt])
        eng = nc.vector if en == "dve" else nc.gpsimd
        return self.c.op(en, lambda: eng.tensor_copy(out=out, in_=in_), reads, [wt])

    def memset(self, en, wt, ap, val):
        eng = self.nc.vector if en == "dve" else self.nc.gpsimd
        return self.c.op(en, lambda: eng.memset(ap, val), [], [wt])

    def recip(self, wt, out, in_, reads):
        nc = self.nc
        return self.c.op("dve", lambda: nc.vector.reciprocal(out=out, in_=in_), reads, [wt])

    def load(self, en, t, out, src_t, in_, par=False):
        eng = {"sp": self.nc.sync, "pool": self.nc.gpsimd, "act": self.nc.scalar}[en]
        if en == "pool":
            return self.c.dma(en, t, src_t, lambda: eng.dma_start(out=out, in_=in_, max_dma_last_dim=2048), par=par)
        return self.c.dma(en, t, src_t, lambda: eng.dma_start(out=out, in_=in_), par=par)

    def rstd_from_ssq(self, ssq, tmp, rstd, n, reads_t):
        self.ts("dve", tmp, tmp[:], ssq[:], 1.0 / n, EPS, ALU.mult, ALU.add, [ssq])
        self.act(tmp, tmp[:], tmp[:], AF.Sqrt, [tmp])
        self.recip(rstd, rstd[:], tmp[:], [tmp])

    def layernorm_rows(self, src_t, src, nrm_t, nrm, st6, mv, tmp, rstd, rms=False):
        nc = self.nc
        self.c.op("dve", lambda: nc.vector.bn_stats(out=st6[:], in_=src), [src_t], [st6])
        self.c.op("dve", lambda: nc.vector.bn_aggr(out=mv[:], in_=st6[:]), [st6], [mv])
        self.ts("dve", tmp, tmp[:], mv[:, 1:2], EPS, None, ALU.add, None, [mv])
        self.act(tmp, tmp[:], tmp[:], AF.Sqrt, [tmp])
        self.recip(rstd, rstd[:], tmp[:], [tmp])
        self.ts("dve", nrm_t, nrm, src, mv[:, 0:1], rstd[:], ALU.subtract, ALU.mult, [src_t, mv, rstd])


class Env:
    def __init__(self, nc=None, c=None, k=None, P=None, tag="", over=None, inner=False, prefix=""):
        self.fused = nc is not None
        self.inner = inner
        self.prefix = prefix
        if nc is None:
            nc = bass.Bass("TRN2", target_bir_lowering=False)
            c = Ctx(nc)
            k = K(nc, c)
            P = [c.ps(f"P{i}") for i in range(8)]
        self.nc, self.c, self.k, self.P = nc, c, k, P
        self.tag = tag
        self.over = over or {}
        self.declared = {}

    def din(self, name, shape, dt=F32):
        if name in self.over:
            return self.over[name]
        return self.nc.dram_tensor(self.tag + name, list(shape), dt, kind="ExternalInput").ap()

    def dout(self, name, shape, dt=F32):
        if name in self.over:
            return self.over[name]
        return self.nc.dram_tensor(self.tag + name, list(shape), dt, kind="ExternalOutput").ap()

    def begin(self):
        self.c.prefix = self.prefix
        if self.fused and not self.inner:
            self.c.begin_phase()

    def setup_done(self):
        self.c.prefix = ""

    def end(self, outs):
        if self.fused:
            if not self.inner:
                self.c.end_phase()
            return None
        self.c.finish(outs)
        return self.nc


def _drain(gen):
    try:
        while True:
            next(gen)
    except StopIteration as e:
        return e.value


def _interleave(gens):
    alive = list(gens)
    while alive:
        for g in list(alive):
            try:
                next(g)
            except StopIteration:
                alive.remove(g)


def _din(nc, name, shape, dt=F32):
    return nc.dram_tensor(name, list(shape), dt, kind="ExternalInput").ap()


def _dout(nc, name, shape, dt=F32):
    return nc.dram_tensor(name, list(shape), dt, kind="ExternalOutput").ap()


NTOK0 = 1540


def build_A0(S, env=None):
    return _drain(gen_A0(S, env))


def gen_A0(S, env=None):
    env = env or Env()
    nc, c, k = env.nc, env.c, env.k
    env.begin()
    NCH = S // 128
    x = env.din("x", [S, 1024])
    wn = env.din("wn", [128, 1024])
    w_tok = env.din("w_tok", [1024, NTOK0])
    w_feat = env.din("w_feat", [1024, 512])
    convw = env.din("convw", [128, 16])
    bif = env.din("bif", [128, 4])
    gnw = env.din("gnw", [128, 512])
    cosd = env.din("cos", [S, 64])
    sind = env.din("sin", [S, 64])
    dmT = env.din("dmT", [128, 256])
    xir = env.din("xir", [128, 256])
    zg = env.din("zg", [128, 4])
    identf = env.din("identf", [128, 128])
    utri = env.din("utri", [128, 128])
    y = env.dout("y", [S, 512])

    D = {n: c.dram(n, a) for n, a in dict(x=x, wn=wn, w_tok=w_tok, w_feat=w_feat, convw=convw, bif=bif, gnw=gnw,
                                            cos=cosd, sin=sind, dmT=dmT, xir=xir, zg=zg, identf=identf,
                                            utri=utri, y=y).items()}
    wn_t = c.sb("wn_t", [128, 1024])
    wtok_t = c.sb("wtok_t", [128, 8, NTOK0], BF16)
    wfeat_t = c.sb("wfeat_t", [128, 8, 512], BF16)
    convw_t = c.sb("convw_t", [128, 16])
    bif_t = c.sb("bif_t", [128, 4])
    gnw_t = c.sb("gnw_t", [128, 512])
    dmT_t = c.sb("dmT_t", [128, 256])
    xir_t = c.sb("xir_t", [128, 256])
    zg_t = c.sb("zg_t", [128, 4])
    idf_t = c.sb("idf_t", [128, 128])
    idb_t = c.sb("idb_t", [128, 128], BF16)
    utri_t = c.sb("utri_t", [128, 128])
    k.load("sp", wn_t, wn_t[:], D["wn"], wn[:, :])
    for kk in range(8):
        k.load("pool", wtok_t, wtok_t[:, kk, :], D["w_tok"], w_tok[kk * 128:(kk + 1) * 128, :], par=True)
        k.load("pool", wfeat_t, wfeat_t[:, kk, :], D["w_feat"], w_feat[kk * 128:(kk + 1) * 128, :], par=True)
    for t, d, a in [(convw_t, "convw", convw), (bif_t, "bif", bif), (gnw_t, "gnw", gnw), (dmT_t, "dmT", dmT),
                    (xir_t, "xir", xir), (zg_t, "zg", zg), (idf_t, "identf", identf), (utri_t, "utri", utri)]:
        k.load("sp", t, t[:], D[d], a[:, :])
    k.cp("dve", idb_t, idb_t[:], idf_t[:], [idf_t])

    xt = [c.sb(f"xt{i}", [128, 1024]) for i in range(2)]
    cs_t = [c.sb(f"cs{i}", [128, 2, 64]) for i in range(2)]
    sq = c.sb("sq", [128, 1024])
    ssq = c.sb("ssq", [128, 1]); tmp1 = c.sb("tmp1", [128, 1]); rstd = c.sb("rstd", [128, 1])
    hn = c.sb("hn", [128, 1024], BF16)
    hnT = c.sb("hnT", [128, 8, 128], BF16)
    ztok = c.sb("ztok", [128, NTOK0])
    zc = c.sb("zc", [128, 4, 131])
    cacc = c.sb("cacc", [128, 4, 128])
    csil = c.sb("csil", [128, 4, 128])
    mqT = c.sb("mqT", [128, 2, 128], BF16)
    mkT = c.sb("mkT", [128, 2, 128], BF16)
    rt1 = c.sb("rt1", [128, 4, 64]); rt2 = c.sb("rt2", [128, 4, 64])
    rot = c.sb("rot", [128, 4, 128], BF16)
    rT = c.sb("rT", [128, 4, 128], BF16)
    rkz = c.sb("rkz", [128, 2, 128], BF16)
    vb = c.sb("vb", [128, 2, 128], BF16)
    St = c.sb("St", [128, 128], BF16)
    rqx = c.sb("rqx", [128, 128], BF16)
    rstate = [c.sb(f"rstate{h}", [128, 128]) for h in range(2)]
    rstate_b = [c.sb(f"rstateb{h}", [128, 128], BF16) for h in range(2)]
    st6 = c.sb("st6", [128, 6]); mv = c.sb("mv", [128, 2]); tmp2 = c.sb("tmp2", [128, 1]); rstd2 = c.sb("rstd2", [128, 1])
    yn = c.sb("yn", [128, 128])
    sg = c.sb("sg", [128, 512])
    yt = [c.sb(f"yt{i}", [128, 512]) for i in range(2)]
    ydst = [c.dram(f"ydst{i}", None) for i in range(2)]
    gpre = c.sb("gpre", [128, 4])
    lf = c.sb("lf", [128, 2]); lft = c.sb("lft", [128, 2])
    lfbc = c.sb("lfbc", [128, 128])
    bcol = c.sb("bcol", [128, 2]); sj = c.sb("sj", [128, 2])
    DT = c.sb("DT", [128, 128]); DTm = c.sb("DTm", [128, 128])
    EB = c.sb("EB", [128, 128])
    blast = c.sb("blast", [128, 1]); wj = c.sb("wj", [128, 1])
    Sd = c.sb("Sd", [128, 128], BF16)
    mqx = c.sb("mqx", [128, 128], BF16)
    vaug = [c.sb(f"vaug{h}", [128, 130], BF16) for h in range(2)]
    kw_t = c.sb("kw_t", [128, 128], BF16)
    Cst = [c.sb(f"Cst{h}", [128, 130]) for h in range(2)]
    Cst_b = [c.sb(f"Cstb{h}", [128, 130], BF16) for h in range(2)]
    den = c.sb("den", [128, 1]); rden = c.sb("rden", [128, 1])
    hh = c.sb("hh", [128, 128])
    P = env.P

    _pn = [0]

    def nps():
        _pn[0] += 1
        return P[_pn[0] % len(P)]

    for h in range(2):
        k.memset("dve", rstate[h], rstate[h][:], 0.0)
        k.memset("dve", rstate_b[h], rstate_b[h][:], 0.0)
        k.memset("dve", Cst[h], Cst[h][:], 0.0)
        k.memset("dve", Cst_b[h], Cst_b[h][:], 0.0)
        k.memset("dve", vaug[h], vaug[h][:], 1.0)
    k.memset("dve", zc, zc[:], 0.0)

    env.setup_done()
    yield
    for ch in range(NCH):
        c.prefix = ""
        X = xt[ch % 2]; CS = cs_t[ch % 2]; Y = yt[ch % 2]
        r0 = ch * 128
        k.load("sp", X, X[:], D["x"], x[r0:r0 + 128, :])
        k.load("sp", CS, CS[:, 0, :], D["cos"], cosd[r0:r0 + 128, :], par=True)
        k.load("sp", CS, CS[:, 1, :], D["sin"], sind[r0:r0 + 128, :], par=True)
        k.act(sq, sq[:], X[:], AF.Square, [X], accum=ssq[:], extra_w=[ssq])
        k.rstd_from_ssq(ssq, tmp1, rstd, 1024.0, None)
        k.stt(hn, hn[:], X[:], rstd[:], wn_t[:], ALU.mult, ALU.mult, [X, rstd, wn_t])
        pt = nps()
        ptb = pt[:].bitcast(BF16)
        for kk in range(8):
            k.tr(pt, ptb[:, kk * 128:(kk + 1) * 128], hn[:, kk * 128:(kk + 1) * 128], idb_t[:], [hn, idb_t])
        k.cp("act", hnT, hnT[:].rearrange("p a b -> p (a b)"), ptb[:, 0:1024], [pt])
        for nb in range(4):
            c0 = nb * 512
            n = min(512, NTOK0 - c0)
            pz = nps()
            for kk in range(8):
                k.mm(pz, pz[:, 0:n], hnT[:, kk, :], wtok_t[:, kk, c0:c0 + n], kk == 0, kk == 7, [hnT, wtok_t])
            if nb % 2 == 0:
                k.cp("act", ztok, ztok[:, c0:c0 + n], pz[:, 0:n], [pz])
            else:
                k.cp("dve", ztok, ztok[:, c0:c0 + n], pz[:, 0:n], [pz])
        pf = nps()
        for j in range(4):
            for kk in range(8):
                k.mm(pf, pf[:, j * 128:(j + 1) * 128], wfeat_t[:, kk, j * 128:(j + 1) * 128], hnT[:, kk, :],
                     kk == 0, kk == 7, [hnT, wfeat_t])
        k.cp("act", zc, zc[:, :, 3:131], pf[:].rearrange("p (a b) -> p a b", a=4), [pf])
        yield
        for j in range(4):
            k.ts("dve", cacc, cacc[:, j, :], zc[:, j, 0:128], convw_t[:, j * 4:j * 4 + 1], None, ALU.mult, None,
                 [zc, convw_t])
            for tp in range(1, 4):
                k.stt(cacc, cacc[:, j, :], zc[:, j, tp:tp + 128], convw_t[:, j * 4 + tp:j * 4 + tp + 1], cacc[:, j, :],
                      ALU.mult, ALU.add, [zc, convw_t, cacc])
        k.cp("dve", zc, zc[:, :, 0:3], zc[:, :, 128:131], [zc])
        k.act(csil, csil[:], cacc[:], AF.Silu, [cacc])
        k.cp("dve", mqT, mqT[:], csil[:, 0:2, :], [csil])
        k.ts("dve", mkT, mkT[:], csil[:, 2:4, :], 128.0 ** -0.5, None, ALU.mult, None, [csil])
        zq = ztok[:, 0:512].rearrange("p (g t d) -> p g t d", g=4, t=2)
        cosb = CS[:, 0:1, :].to_broadcast([128, 4, 64])
        sinb = CS[:, 1:2, :].to_broadcast([128, 4, 64])
        k.tt("dve", rt1, rt1[:], zq[:, :, 0, :], cosb, ALU.mult, [ztok, CS])
        k.tt("pool", rt2, rt2[:], zq[:, :, 1, :], sinb, ALU.mult, [ztok, CS])
        k.tt("dve", rot, rot[:, :, 0:64], rt1[:], rt2[:], ALU.subtract, [rt1, rt2])
        k.tt("pool", rt1, rt1[:], zq[:, :, 1, :], cosb, ALU.mult, [ztok, CS])
        k.tt("dve", rt2, rt2[:], zq[:, :, 0, :], sinb, ALU.mult, [ztok, CS])
        k.tt("dve", rot, rot[:, :, 64:128], rt1[:], rt2[:], ALU.add, [rt1, rt2, rot])
        pr = nps()
        prb = pr[:].bitcast(BF16)
        for g4 in range(4):
            k.tr(pr, prb[:, g4 * 128:(g4 + 1) * 128], rot[:, g4, :], idb_t[:], [rot, idb_t])
        k.cp("act", rT, rT[:].rearrange("p a b -> p (a b)"), prb[:, 0:512], [pr])
        for h in range(2):
            k.ts("dve", rkz, rkz[:, h, :], rot[:, 2 + h, :], zg_t[:, h:h + 1], None, ALU.mult, None, [rot, zg_t])
        k.cp("dve", vb, vb[:].rearrange("p a b -> p (a b)"), ztok[:, 512:768], [ztok])
        k.act(sg, sg[:, 0:256], ztok[:, 768:1024], AF.Silu, [ztok])
        k.act(sg, sg[:, 256:512], ztok[:, 1280:1536], AF.Sigmoid, [ztok, sg])
        yield
        for h in range(2):
            pS = nps()
            k.mm(pS, pS[:, 0:128], rT[:, 2 + h, :], rT[:, h, :], True, True, [rT])
            k.tt("dve", St, St[:], pS[:, 0:128], dmT_t[:, h * 128:(h + 1) * 128], ALU.mult, [pS, dmT_t])
            k.tt("pool", rqx, rqx[:], rT[:, h, :], xir_t[:, h * 128:(h + 1) * 128], ALU.mult, [rT, xir_t])
            pO = nps()
            k.mm(pO, pO[:, 0:128], St[:], vb[:, h, :], True, False, [St, vb])
            k.mm(pO, pO[:, 0:128], rqx[:], rstate_b[h][:], False, True, [rqx, rstate_b[h]])
            pU = nps()
            k.mm(pU, pU[:, 0:128], rkz[:, h, :], vb[:, h, :], True, True, [rkz, vb])
            k.stt(rstate[h], rstate[h][:], rstate[h][:], zg_t[:, 2 + h:3 + h], pU[:, 0:128], ALU.mult, ALU.add,
                  [rstate[h], zg_t, pU])
            k.cp("act", rstate_b[h], rstate_b[h][:], rstate[h][:], [rstate[h]])
            k.layernorm_rows(pO, pO[:, 0:128], yn, yn[:], st6, mv, tmp2, rstd2)
            k.tt("dve", yn, yn[:], yn[:], gnw_t[:, h * 128:(h + 1) * 128], ALU.mult, [yn, gnw_t])
            k.tt("dve", Y, Y[:, h * 128:(h + 1) * 128], yn[:], sg[:, h * 128:(h + 1) * 128], ALU.mult, [yn, sg])
        yield
        k.tt("dve", gpre, gpre[:], ztok[:, 1536:1540], bif_t[:], ALU.add, [ztok, bif_t])
        k.act(lft, lft[:], gpre[:, 2:4], AF.Exp, [gpre], scale=-1.0)
        k.act(lft, lft[:], lft[:], AF.Ln, [lft], bias=1.0)
        k.ts("dve", lf, lf[:], lft[:], -1.0, None, ALU.mult, None, [lft])
        pb = nps()
        k.mm(pb, pb[:, 0:2], utri_t[:], lf[:], True, True, [utri_t, lf])
        k.cp("dve", bcol, bcol[:], pb[:, 0:2], [pb])
        k.tt("dve", sj, sj[:], gpre[:, 0:2], bcol[:], ALU.subtract, [gpre, bcol])
        for h in range(2):
            k.cp("dve", lfbc, lfbc[:], lf[:, h:h + 1].to_broadcast([128, 128]), [lf])
            pB = nps()
            k.mm(pB, pB[:, 0:128], lfbc[:], utri_t[:], True, True, [lfbc, utri_t])
            k.act(DT, DT[:], pB[:, 0:128], AF.Exp, [pB, sj], bias=sj[:, h:h + 1])
            c.op("pool", lambda: nc.gpsimd.affine_select(out=DTm[:], in_=DT[:], pattern=[[1, 128]],
                                                          compare_op=ALU.is_ge, fill=0.0, base=0,
                                                          channel_multiplier=-1), [DT], [DTm])
            k.act(EB, EB[:], pB[:, 0:128], AF.Exp, [pB])
            k.cp("dve", blast, blast[:], pB[:, 127:128], [pB])
            k.act(wj, wj[:], sj[:, h:h + 1], AF.Exp, [sj, blast], bias=blast[:])
            pS = nps()
            k.mm(pS, pS[:, 0:128], mkT[:, h, :], mqT[:, h, :], True, True, [mkT, mqT])
            k.tt("dve", Sd, Sd[:], pS[:, 0:128], DTm[:], ALU.mult, [pS, DTm])
            k.tt("pool", mqx, mqx[:], mqT[:, h, :], EB[:], ALU.mult, [mqT, EB])
            k.cp("act", vaug[h], vaug[h][:, 0:128], ztok[:, 1024 + h * 128:1024 + (h + 1) * 128], [ztok])
            pN = nps()
            k.mm(pN, pN[:, 0:130], Sd[:], vaug[h][:], True, False, [Sd, vaug[h]])
            k.mm(pN, pN[:, 0:130], mqx[:], Cst_b[h][:], False, True, [mqx, Cst_b[h]])
            yield
            pk = nps()
            pkb = pk[:].bitcast(BF16)
            k.tr(pk, pkb[:, 0:128], mkT[:, h, :], idb_t[:], [mkT, idb_t])
            k.ts("dve", kw_t, kw_t[:], pkb[:, 0:128], wj[:], None, ALU.mult, None, [pk, wj])
            pC = nps()
            k.mm(pC, pC[:, 0:130], kw_t[:], vaug[h][:], True, True, [kw_t, vaug[h]])
            k.stt(Cst[h], Cst[h][:], Cst[h][:], EB[:, 127:128], pC[:, 0:130], ALU.mult, ALU.add, [Cst[h], EB, pC])
            k.cp("act", Cst_b[h], Cst_b[h][:], Cst[h][:], [Cst[h]])
            k.act(den, den[:], pN[:, 128:129], AF.Abs, [pN])
            k.ts("dve", den, den[:], den[:], 1.0, None, ALU.max, None, [den])
            k.recip(rden, rden[:], den[:], [den])
            k.ts("dve", hh, hh[:], pN[:, 0:128], rden[:], None, ALU.mult, None, [pN, rden])
            k.layernorm_rows(hh, hh[:], yn, yn[:], st6, mv, tmp2, rstd2)
            k.tt("dve", yn, yn[:], yn[:], gnw_t[:, 256 + h * 128:256 + (h + 1) * 128], ALU.mult, [yn, gnw_t])
            k.tt("dve", Y, Y[:, 256 + h * 128:256 + (h + 1) * 128], yn[:], sg[:, 256 + h * 128:256 + (h + 1) * 128],
                 ALU.mult, [yn, sg])
        c.dma("sp", ydst[ch % 2], Y, lambda: nc.sync.dma_start(out=y[r0:r0 + 128, :], in_=Y[:]))
        yield
    return env.end(ydst)


def consts_A0():
    H = 4
    log_g = np.log(1.0 - 2.0 ** (-5.0 - np.arange(H, dtype=np.float32))).astype(np.float32)
    idx = np.arange(128, dtype=np.float32)
    diff = idx[:, None] - idx[None, :]
    causal = diff >= 0
    dmat = np.where(causal[None], np.exp(np.where(causal, diff, 0.0)[None] * log_g[:, None, None]), 0.0).astype(np.float32)
    xi = np.exp((idx + 1.0)[None, :] * log_g[:, None]).astype(np.float32)
    zeta = np.exp((128 - 1.0 - idx)[None, :] * log_g[:, None]).astype(np.float32)
    g_chunk = np.exp(128 * log_g).astype(np.float32)
    return dmat, xi, zeta, g_chunk


def rope_tables(S):
    d = 128
    pos = np.arange(S, dtype=np.float32)
    inv = (1.0 / (np.float32(10000.0) ** (np.arange(0, d, 2, dtype=np.float32) / np.float32(d)))).astype(np.float32)
    ang = (pos[:, None] * inv[None, :]).astype(np.float32)
    return np.cos(ang).astype(np.float32), np.sin(ang).astype(np.float32)


def inputs_A0(x, norm_w, w_in, conv_w, b_i, b_f, gn_ret, gn_ml):
    B, S, _ = x.shape
    dmat, xi, zeta, g_chunk = consts_A0()
    cos, sin = rope_tables(S)
    sc = np.float32(128.0 ** -0.5)
    identf = np.eye(128, dtype=np.float32)
    utri = np.triu(np.ones((128, 128), dtype=np.float32))
    maps = []
    for core in range(8):
        b, g = core // 2, core % 2
        hs = [2 * g, 2 * g + 1]
        def cols(base, width=128):
            return np.concatenate([np.arange(base + h * width, base + (h + 1) * width) for h in hs])
        tok_cols = np.concatenate([cols(0), cols(512), cols(1024), cols(1536), cols(3072), cols(3584),
                                   np.array([4096 + h for h in hs]), np.array([4100 + h for h in hs])])
        feat_cols = np.concatenate([cols(2048), cols(2560)])
        cw = conv_w[:, feat_cols - 2048]
        convw = np.ascontiguousarray(cw.reshape(4, 4, 128).transpose(2, 1, 0).reshape(128, 16))
        bif = np.tile(np.concatenate([b_i[hs], b_f[hs]])[None, :], (128, 1)).astype(np.float32)
        gnw = np.tile(np.concatenate([gn_ret[cols(0)], gn_ml[cols(0)]])[None, :], (128, 1)).astype(np.float32)
        dmT = np.concatenate([dmat[h].T * sc for h in hs], axis=1).astype(np.float32)
        xir = np.concatenate([np.tile(xi[h][None, :], (128, 1)) for h in hs], axis=1).astype(np.float32)
        zg = np.stack([zeta[hs[0]] * sc, zeta[hs[1]] * sc, np.full(128, g_chunk[hs[0]]), np.full(128, g_chunk[hs[1]])],
                      axis=1).astype(np.float32)
        maps.append(dict(
            x=np.ascontiguousarray(x[b]), wn=np.tile(norm_w[None, :], (128, 1)).astype(np.float32),
            w_tok=np.ascontiguousarray(w_in[:, tok_cols]), w_feat=np.ascontiguousarray(w_in[:, feat_cols]),
            convw=convw, bif=bif, gnw=gnw, cos=cos, sin=sin, dmT=np.ascontiguousarray(dmT), xir=np.ascontiguousarray(xir),
            zg=np.ascontiguousarray(zg), identf=identf, utri=utri))
    return maps


def gather_A0(results, B, S):
    mixed = np.empty((B, S, 1024), dtype=np.float32)
    for core in range(8):
        b, g = core // 2, core % 2
        yv = results[core]["y"]
        for j, h in enumerate([2 * g, 2 * g + 1]):
            mixed[b, :, h * 128:(h + 1) * 128] = yv[:, j * 128:(j + 1) * 128]
            mixed[b, :, 512 + h * 128:512 + (h + 1) * 128] = yv[:, 256 + j * 128:256 + (j + 1) * 128]
    return mixed


TB = 256
SCENG = ['dve', 'act']


def build_B(NTOK, final, NI1=128, dbg=99, preconv=True, env=None, write_y=True, write_yf=True):
    env = env or Env()
    nc, c, k = env.nc, env.c, env.k
    env.begin()
    NB = NTOK // TB
    hin = env.din("hin", [NTOK, 1024])
    mixed = env.din("mixed", [NTOK, 1024])
    pin = env.din("pin", [NTOK, 256])
    wout = env.din("wout", [2, 128, 8, 512])
    wgate = env.din("wgate", [2, 128, 8, 512])
    wq = env.din("wq", [16, 128, 8, 128])
    wproj = env.din("wproj", [128, 2, 1024])
    wnorm = env.din("wnorm", [128, 3, 1024])
    skT = env.din("skT", [128, 2, 128])
    ul = env.din("ul", [NI1, 128, 8 * 128])
    vl = env.din("vl", [NI1, 128, 1024])
    identf = env.din("identf", [128, 128])
    iota = env.din("iota", [128, 128])
    y = env.dout("y", [NTOK, 1024]) if write_y else None
    yf = env.dout("yf", [NTOK, 1024]) if write_yf else None
    if "ub" in env.over:
        ub, vbd = env.over["ub"], env.over["vbd"]
    else:
        ub = nc.dram_tensor("ub", [NI1, 128, 1024], BF16, kind="Internal").ap()
        vbd = nc.dram_tensor("vbd", [NI1, 128, 1024], BF16, kind="Internal").ap()
    D = {n: c.dram(n, a) for n, a in dict(hin=hin, mixed=mixed, pin=pin, wout=wout, wgate=wgate, wq=wq, wproj=wproj,
                                            wnorm=wnorm, skT=skT, ul=ul, vl=vl, identf=identf, iota=iota, y=y,
                                            yf=yf, ub=ub, vbd=vbd).items()}
    wn_t = c.sb("wn_t", [128, 3, 1024])
    wproj_t = c.sb("wproj_t", [128, 2, 1024], BF16)
    skT_t = c.sb("skT_t", [128, 2, 128], BF16)
    idf_t = c.sb("idf_t", [128, 128])
    idb_t = c.sb("idb_t", [128, 128], BF16)
    iota_t = c.sb("iota_t", [128, 128])
    k.load("sp", wn_t, wn_t[:], D["wnorm"], wnorm[:, :, :])
    k.load("pool", wproj_t, wproj_t[:], D["wproj"], wproj[:, :, :])
    k.load("pool", skT_t, skT_t[:], D["skT"], skT[:, :, :])
    k.load("sp", idf_t, idf_t[:], D["identf"], identf[:, :])
    k.load("sp", iota_t, iota_t[:], D["iota"], iota[:, :])
    k.cp("dve", idb_t, idb_t[:], idf_t[:], [idf_t])
    uc = [c.sb(f"uc{i}", [128, 8, 128], BF16) for i in range(3)]
    vc = [c.sb(f"vc{i}", [128, 1024], BF16) for i in range(3)]
    stg_main = [uc[i] for i in range(3)] + [vc[i] for i in range(3)]
    stg = [(Tile(c, f"stgU{i}", uc[i].h), uc[i][:].rearrange("p a b -> p (a b)")) for i in range(3)] + \
          [(Tile(c, f"stgV{i}", vc[i].h), vc[i][:]) for i in range(3)]
    stq = [c.dram(f"stq{i}", None) for i in range(6)]
    tabt = {"ub": [], "vbd": []}
    n = 0
    for src, srcn, dst, dstn in [(ul, "ul", ub, "ub"), (vl, "vl", vbd, "vbd")]:
        for i1 in range(NI1 if preconv else 0):
            (s_, sap) = stg[n % 6]; q_ = stq[n % 6]; n += 1
            k.load("pool", s_, sap, D[srcn], src[i1, :, :])
            c.dma("sp", q_, s_, (lambda dst=dst, i1=i1, sap=sap: nc.sync.dma_start(out=dst[i1, :, :], in_=sap)))
            tt_ = c.dram(f"{dstn}{i1}", None)
            tt_.last_w = list(q_.last_w)
            tabt[dstn].append(tt_)
    wA = [c.sb(f"wA{i}", [128, 8, 512], BF16) for i in range(2)]
    wQ = [c.sb(f"wQ{i}", [128, 8, 128], BF16) for i in range(2)]
    h1s = [c.sb(f"h1_{i}", [128, 2, 1024]) for i in range(2)]
    mx = c.sb("mx", [128, 1024])
    mb = c.sb("mb", [128, 1024], BF16)
    mT = c.sb("mT", [128, 8, 128], BF16)
    sq = c.sb("sq", [128, 1024], BF16)
    ssq = c.sb("ssq", [128, 1]); tmp1 = c.sb("tmp1", [128, 1]); rstd = c.sb("rstd", [128, 1])
    hnb = c.sb("hnb", [128, 1024], BF16)
    hnTs = [c.sb(f"hnT{i}", [128, 8, TB], BF16) for i in range(2)]
    qTj = [c.sb(f"qTj{i}", [128, TB], BF16) for i in range(2)]
    sc = [c.sb(f"sc{i}", [128, 16, 128]) for i in range(2)]
    scw = c.sb("scw", [128, 128])
    top = c.sb("top", [128, 16, 16])
    itu = c.sb("itu", [128, 16, 16], U32)
    itf = c.sb("itf", [128, 16, 16])
    cand = c.sb("cand", [128, 8, 256])
    candw = c.sb("candw", [128, 256])
    eful = c.sb("eful", [128, 8, 256])
    best = c.sb("best", [128, 8, 16])
    junk = c.sb("junk", [128, 256])
    eid = c.sb("eid", [128, 128])
    eii = c.sb("eii", [128, 128], I32)
    ei1 = c.sb("ei1", [128, 128], I32); ei2 = c.sb("ei2", [128, 128], I32)
    a1 = c.sb("a1", [128, 128]); a2 = c.sb("a2", [128, 128])
    gat = c.sb("gat", [128, 8, 16]); gsum = c.sb("gsum", [128, 8]); grs = c.sb("grs", [128, 8])
    a1T = c.sb("a1T", [128, TB]); a2T = c.sb("a2T", [128, TB]); gT = c.sb("gT", [128, TB])
    A2 = [c.sb(f"A2_{i}", [128, 128], BF16) for i in range(2)]
    A1 = [c.sb(f"A1_{i}", [128, 128], BF16) for i in range(2)]
    WT = c.sb("WT", [128, 128, TB], BF16)
    ge = [c.sb(f"ge{i}", [128, TB]) for i in range(2)]
    Gt = [c.sb(f"Gt{i}", [128, TB], BF16) for i in range(2)]
    for (al, _), mt in zip(stg, stg_main):
        mt.last_w = list(al.last_w)
        mt.reads = list(al.reads)
    pt_in = c.sb("pt_in", [128, 256]); ptb = c.sb("ptb", [128, 256], BF16); pT = c.sb("pT", [128, 2, 128], BF16)
    sig = c.sb("sig", [128, 512]); tmpe = c.sb("tmpe", [128, 512])
    P = env.P
    PO = P[0:4]
    PA = P[4:6]
    PR = P[6:8]

    def nps():
        c._psn += 1
        return PR[c._psn % 2]

    def rmsnorm_to_T(src_t, src, widx, tt, hnT):
        k.act(sq, sq[:], src, AF.Square, [src_t], accum=ssq[:], extra_w=[ssq])
        k.rstd_from_ssq(ssq, tmp1, rstd, 1024.0, None)
        k.stt(hnb, hnb[:], src, rstd[:], wn_t[:, widx, :], ALU.mult, ALU.mult, [src_t, rstd, wn_t])
        pt = nps()
        ptv = pt[:].bitcast(BF16)
        for kk in range(8):
            k.tr(pt, ptv[:, kk * 128:(kk + 1) * 128], hnb[:, kk * 128:(kk + 1) * 128], idb_t[:], [hnb, idb_t])
        k.cp("act", hnT, hnT[:, :, tt * 128:(tt + 1) * 128], ptv[:, 0:1024].rearrange("p (a b) -> p a b", a=8), [pt])

    def _dbg_out(blk):
        for tt in range(2):
            r0 = blk * TB + tt * 128
            k.cp("act", mx, mx[:], h1[:, tt, :], [h1])
            c.dma("sp", D["y"], mx, (lambda r0=r0: nc.sync.dma_start(out=y[r0:r0 + 128, :], in_=mx[:])), par=True)

    wa_n = [0]

    def prep(blk):
        t0 = blk * TB
        h1 = h1s[blk % 2]; hnT = hnTs[blk % 2]
        for tt in range(2):
            r0 = t0 + tt * 128
            k.load("sp", h1, h1[:, tt, :], D["hin"], hin[r0:r0 + 128, :], par=(tt == 1))
        for tt in range(2):
            r0 = t0 + tt * 128
            k.load("sp", mx, mx[:], D["mixed"], mixed[r0:r0 + 128, :])
            k.cp("dve", mb, mb[:], mx[:], [mx])
            pt = nps()
            ptv = pt[:].bitcast(BF16)
            for kk in range(8):
                k.tr(pt, ptv[:, kk * 128:(kk + 1) * 128], mb[:, kk * 128:(kk + 1) * 128], idb_t[:], [mb, idb_t])
            k.cp("act", mT, mT[:].rearrange("p a b -> p (a b)"), ptv[:, 0:1024], [pt])
            for half in range(2):
                W = wA[wa_n[0] % 2]; wa_n[0] += 1
                k.load("pool", W, W[:], D["wout"], wout[half, :, :, :])
                pz = nps()
                for kk in range(8):
                    k.mm(pz, pz[:, :], mT[:, kk, :], W[:, kk, :], kk == 0, kk == 7, [mT, W])
                k.tt("dve", h1, h1[:, tt, half * 512:(half + 1) * 512], pz[:, :], h1[:, tt, half * 512:(half + 1) * 512],
                     ALU.add, [pz, h1])
                yield
        for tt in range(2):
            rmsnorm_to_T(h1, h1[:, tt, :], 0, tt, hnT)
            yield
        for j in range(16):
            Wj = wQ[j % 2]
            k.load("pool", Wj, Wj[:], D["wq"], wq[j, :, :, :])
            pq = nps()
            for kk in range(8):
                k.mm(pq, pq[:, 0:TB], Wj[:, kk, :], hnT[:, kk, :], kk == 0, kk == 7, [Wj, hnT])
            Q = qTj[j % 2]
            k.cp("act", Q, Q[:], pq[:, 0:TB], [pq])
            psc = nps()
            for tt in range(2):
                k.mm(psc, psc[:, tt * 128:(tt + 1) * 128], Q[:, tt * 128:(tt + 1) * 128], skT_t[:, j % 2, :], True, True,
                     [Q, skT_t])
            for tt in range(2):
                k.cp(SCENG[tt], sc[tt], sc[tt][:, j, :], psc[:, tt * 128:(tt + 1) * 128], [psc])
            yield
        for tt in range(2):
            S_ = sc[tt]
            for j in range(16):
                c.op("dve", lambda: nc.vector.max(out=top[:, j, 0:8], in_=S_[:, j, :]), [S_], [top])
                c.op("dve", lambda: nc.vector.max_index(out=itu[:, j, 0:8], in_max=top[:, j, 0:8], in_values=S_[:, j, :]),
                     [S_, top], [itu])
                c.op("dve", lambda: nc.vector.match_replace(out=scw[:], in_to_replace=top[:, j, 0:8], in_values=S_[:, j, :],
                                                            imm_value=-1e30), [S_, top], [scw])
                c.op("dve", lambda: nc.vector.max(out=top[:, j, 8:16], in_=scw[:]), [scw], [top])
                c.op("dve", lambda: nc.vector.max_index(out=itu[:, j, 8:16], in_max=top[:, j, 8:16], in_values=scw[:]),
                     [scw, top], [itu])
                yield
            k.cp("dve", itf, itf[:], itu[:], [itu])
            topv = top[:].rearrange("p (h two) a -> p h two a", two=2)
            itfv = itf[:].rearrange("p (h two) a -> p h two a", two=2)
            candv = cand[:].rearrange("p h (a b) -> p h a b", a=16)
            efulv = eful[:].rearrange("p h (a b) -> p h a b", a=16)
            k.tt("dve", cand, candv, topv[:, :, 0, :].unsqueeze(3).to_broadcast([128, 8, 16, 16]),
                 topv[:, :, 1, :].unsqueeze(2).to_broadcast([128, 8, 16, 16]), ALU.add, [top])
            k.ts("dve", itf, itfv[:, :, 0, :], itfv[:, :, 0, :], 128.0, None, ALU.mult, None, [itf])
            k.tt("dve", eful, efulv, itfv[:, :, 0, :].unsqueeze(3).to_broadcast([128, 8, 16, 16]),
                 itfv[:, :, 1, :].unsqueeze(2).to_broadcast([128, 8, 16, 16]), ALU.add, [itf])
            for h in range(8):
                c.op("dve", lambda: nc.vector.max(out=best[:, h, 0:8], in_=cand[:, h, :]), [cand], [best])
                c.op("dve", lambda: nc.vector.match_replace(out=candw[:], in_to_replace=best[:, h, 0:8],
                                                            in_values=cand[:, h, :], imm_value=-1e30), [cand, best], [candw])
                c.op("dve", lambda: nc.vector.max(out=best[:, h, 8:16], in_=candw[:]), [candw], [best])
                yield
            for h in range(8):
                for kq in range(16):
                    hk = h * 16 + kq
                    k.stt(junk, junk[:], cand[:, h, :], best[:, h, kq:kq + 1], eful[:, h, :], ALU.is_equal, ALU.mult,
                          [cand, best, eful], accum=eid[:, hk:hk + 1], extra_w=[eid])
                    if kq % 8 == 7:
                        yield
            k.cp("dve", eii, eii[:], eid[:], [eid])
            k.ts("dve", ei1, ei1[:], eii[:], 7, None, ALU.arith_shift_right, None, [eii])
            k.ts("dve", ei2, ei2[:], eii[:], 127, None, ALU.bitwise_and, None, [eii])
            k.cp("dve", a1, a1[:], ei1[:], [ei1])
            k.cp("dve", a2, a2[:], ei2[:], [ei2])
            k.tt("dve", gat, gat[:], best[:], best[:, :, 0:1].to_broadcast([128, 8, 16]), ALU.subtract, [best])
            k.act(gat, gat[:], gat[:], AF.Exp, [gat])
            c.op("dve", lambda: nc.vector.tensor_reduce(out=gsum[:], in_=gat[:], axis=AX.X, op=ALU.add), [gat], [gsum])
            k.recip(grs, grs[:], gsum[:], [gsum])
            k.tt("dve", gat, gat[:], gat[:], grs[:].unsqueeze(2).to_broadcast([128, 8, 16]), ALU.mult, [gat, grs])
            for src_t, src, dst in [(a1, a1[:], a1T), (a2, a2[:], a2T), (gat, gat[:].rearrange("p h k -> p (h k)"), gT)]:
                ptr = nps()
                k.tr(ptr, ptr[:, 0:128], src, idf_t[:], [src_t, idf_t])
                k.cp("act", dst, dst[:, tt * 128:(tt + 1) * 128], ptr[:, 0:128], [ptr])
        yield

    for _ in prep(0):
        pass
    for blk in range(NB):
        t0 = blk * TB
        h1 = h1s[blk % 2]; hnT = hnTs[blk % 2]
        for tg in range(TB // 4):
            pw = nps()
            for tq in range(4):
                t = tg * 4 + tq
                A2_ = A2[t % 2]; A1_ = A1[t % 2]
                k.ts("dve", A2_, A2_[:], iota_t[:], a2T[:, t:t + 1], None, ALU.is_equal, None, [iota_t, a2T])
                k.ts("dve", A1_, A1_[:], iota_t[:], a1T[:, t:t + 1], gT[:, t:t + 1], ALU.is_equal, ALU.mult,
                     [iota_t, a1T, gT])
                k.mm(pw, pw[:, tq * 128:(tq + 1) * 128], A2_[:], A1_[:], True, True, [A2_, A1_])
            k.cp("act", WT, WT[:, :, tg * 4:tg * 4 + 4], pw[:, :].rearrange("p (t i) -> p i t", t=4), [pw])
        gnext = prep(blk + 1) if blk + 1 < NB else None
        def emit_A(i1):
            U = uc[i1 % 3]; V = vc[i1 % 3]
            k.load("sp", U, U[:].rearrange("p a b -> p (a b)"), tabt["ub"][i1], ub[i1, :, :])
            k.load("sp", V, V[:], tabt["vbd"][i1], vbd[i1, :, :])
            pa = PA[i1 % 2]
            for kk in range(8):
                k.mm(pa, pa[:, 0:TB], U[:, kk, :], hnT[:, kk, :], kk == 0, kk == 7, [U, hnT])
            return pa, V

        cur = emit_A(0)
        for i1 in range(NI1):
            nxt = emit_A(i1 + 1) if i1 + 1 < NI1 else None
            if gnext is not None:
                next(gnext, None)
            pa, V = cur
            G_ = ge[i1 % 2]; Gb = Gt[i1 % 2]
            k.act(G_, G_[:], pa[:, 0:TB], AF.Gelu, [pa])
            k.tt("dve", Gb, Gb[:], G_[:], WT[:, i1, :], ALU.mult, [G_, WT])
            for tt in range(2):
                for half in range(2):
                    po = PO[tt * 2 + half]
                    k.mm(po, po[:, :], Gb[:, tt * 128:(tt + 1) * 128], V[:, half * 512:(half + 1) * 512], i1 == 0,
                         i1 == NI1 - 1, [Gb, V])
            cur = nxt
        if gnext is not None:
            for _ in gnext:
                pass
        for tt in range(2):
            for half in range(2):
                po = PO[tt * 2 + half]
                k.tt("dve", h1, h1[:, tt, half * 512:(half + 1) * 512], po[:, :], h1[:, tt, half * 512:(half + 1) * 512],
                     ALU.add, [po, h1])
        for tt in range(2):
            rmsnorm_to_T(h1, h1[:, tt, :], 1, tt, hnT)
        for tt in range(2):
            r0 = t0 + tt * 128
            k.load("sp", pt_in, pt_in[:], D["pin"], pin[r0:r0 + 128, :])
            k.cp("dve", ptb, ptb[:], pt_in[:], [pt_in])
            pp = nps()
            ppv = pp[:].bitcast(BF16)
            for kk in range(2):
                k.tr(pp, ppv[:, kk * 128:(kk + 1) * 128], ptb[:, kk * 128:(kk + 1) * 128], idb_t[:], [ptb, idb_t])
            k.cp("act", pT, pT[:].rearrange("p a b -> p (a b)"), ppv[:, 0:256], [pp])
            for half in range(2):
                W = wA[wa_n[0] % 2]; wa_n[0] += 1
                k.load("pool", W, W[:], D["wgate"], wgate[half, :, :, :])
                pg = nps()
                for kk in range(8):
                    k.mm(pg, pg[:, :], hnT[:, kk, tt * 128:(tt + 1) * 128], W[:, kk, :], kk == 0, kk == 7, [hnT, W])
                k.act(sig, sig[:], pg[:, :], AF.Sigmoid, [pg])
                pe_ = nps()
                for kk in range(2):
                    k.mm(pe_, pe_[:, :], pT[:, kk, :], wproj_t[:, kk, half * 512:(half + 1) * 512], kk == 0, kk == 1,
                         [pT, wproj_t])
                k.tt("dve", tmpe, tmpe[:], pe_[:, :], sig[:], ALU.mult, [pe_, sig])
                k.tt("dve", h1, h1[:, tt, half * 512:(half + 1) * 512], tmpe[:], h1[:, tt, half * 512:(half + 1) * 512],
                     ALU.add, [tmpe, h1])
            if write_y:
                c.dma("sp", D["y"], h1, (lambda r0=r0, tt=tt: nc.sync.dma_start(out=y[r0:r0 + 128, :], in_=h1[:, tt, :])), par=True)
            if write_yf:
                k.act(sq, sq[:], h1[:, tt, :], AF.Square, [h1], accum=ssq[:], extra_w=[ssq])
                k.rstd_from_ssq(ssq, tmp1, rstd, 1024.0, None)
                k.stt(mx, mx[:], h1[:, tt, :], rstd[:], wn_t[:, 2, :], ALU.mult, ALU.mult, [h1, rstd, wn_t])
                c.dma("sp", D["yf"], mx, (lambda r0=r0: nc.sync.dma_start(out=yf[r0:r0 + 128, :], in_=mx[:])), par=True)
    return env.end([D["y"], D["yf"]])


def consts_B(w_out, w_gate, w_q, w_proj, wn_ffn, wn_ple, wn_fin, sub_keys, u_tab, v_tab):
    def halves(w):
        return np.ascontiguousarray(w.reshape(8, 128, 2, 512).transpose(2, 1, 0, 3))
    wq = np.ascontiguousarray(w_q.reshape(8, 128, 16, 128).transpose(2, 1, 0, 3))
    wproj = np.ascontiguousarray(w_proj.reshape(2, 128, 1024).transpose(1, 0, 2))
    wnorm = np.ascontiguousarray(np.tile(np.stack([wn_ffn, wn_ple, wn_fin], axis=0)[None], (128, 1, 1))).astype(np.float32)
    skT = np.ascontiguousarray(sub_keys.transpose(2, 0, 1))
    ul = np.ascontiguousarray(u_tab.reshape(128, 128, 8, 128).transpose(0, 3, 2, 1)).reshape(128, 128, 1024)
    vl = np.ascontiguousarray(v_tab.reshape(128, 128, 1024))
    return dict(wout=halves(w_out), wgate=halves(w_gate), wq=wq, wproj=wproj, wnorm=wnorm, skT=skT, ul=ul, vl=vl,
                identf=np.eye(128, dtype=np.float32),
                iota=np.ascontiguousarray(np.tile(np.arange(128, dtype=np.float32)[None, :], (128, 1))))


A1_G1 = F32
A1_G2 = BF16


def build_A1(S, env=None):
    return _drain(gen_A1(S, env))


def gen_A1(S, env=None):
    env = env or Env()
    nc, c, k = env.nc, env.c, env.k
    env.begin()
    NCH = S // 128
    NH = 4
    x = env.din("x", [S, 1024])
    wn = env.din("wn", [128, 1024])
    w_tok = env.din("w_tok", [1024, 520])
    w_feat = env.din("w_feat", [1024, 1536])
    convw = env.din("convw", [128, 48])
    adt = env.din("adt", [128, 8])
    nw = env.din("nw", [128, 128])
    identf = env.din("identf", [128, 128])
    utri = env.din("utri", [128, 128])
    masks = env.din("masks", [128, 384])
    y = env.dout("y", [S, 512])
    D = {n: c.dram(n, a) for n, a in dict(x=x, wn=wn, w_tok=w_tok, w_feat=w_feat, convw=convw, adt=adt, nw=nw,
                                            identf=identf, utri=utri, masks=masks, y=y).items()}
    wn_t = c.sb("wn_t", [128, 1024])
    wtok_t = c.sb("wtok_t", [128, 8, 520], BF16)
    wfeat_t = c.sb("wfeat_t", [128, 8, 1536], BF16)
    convw_t = c.sb("convw_t", [128, 48])
    adt_t = c.sb("adt_t", [128, 8])
    nw_t = c.sb("nw_t", [128, 128])
    idf_t = c.sb("idf_t", [128, 128])
    idb_t = c.sb("idb_t", [128, 128], BF16)
    utri_t = c.sb("utri_t", [128, 128])
    masks_t = c.sb("masks_t", [128, 384])
    ones_t = c.sb("ones_t", [128, 128])
    negA = c.sb("negA", [128, 4])
    k.load("sp", wn_t, wn_t[:], D["wn"], wn[:, :])
    for kk in range(8):
        k.load("pool", wtok_t, wtok_t[:, kk, :], D["w_tok"], w_tok[kk * 128:(kk + 1) * 128, :], par=True)
        k.load("pool", wfeat_t, wfeat_t[:, kk, :], D["w_feat"], w_feat[kk * 128:(kk + 1) * 128, :], par=True)
    for t, d, a in [(convw_t, "convw", convw), (adt_t, "adt", adt), (nw_t, "nw", nw), (idf_t, "identf", identf),
                    (utri_t, "utri", utri), (masks_t, "masks", masks)]:
        k.load("sp", t, t[:], D[d], a[:, :])
    k.cp("dve", idb_t, idb_t[:], idf_t[:], [idf_t])
    k.memset("dve", ones_t, ones_t[:], 1.0)
    k.act(negA, negA[:], adt_t[:, 0:4], AF.Exp, [adt_t])
    k.ts("dve", negA, negA[:], negA[:], -1.0, None, ALU.mult, None, [negA])

    xt = [c.sb(f"xt{i}", [128, 1024]) for i in range(2)]
    sq = c.sb("sq", [128, 1024], BF16)
    ssq = c.sb("ssq", [128, 1]); tmp1 = c.sb("tmp1", [128, 1]); rstd = c.sb("rstd", [128, 1])
    hn = c.sb("hn", [128, 1024], BF16)
    hnT = c.sb("hnT", [128, 8, 128], BF16)
    ztok_2 = [c.sb(f"ztok{i}", [128, 520]) for i in range(2)]
    zc = c.sb("zc", [128, 12, 131])
    cacc = c.sb("cacc", [128, 12, 128])
    csil = c.sb("csil", [128, 12, 128])
    sq2 = c.sb("sq2", [128, 8, 128])
    rn = c.sb("rn", [128, 8, 128])
    qnT_2 = [c.sb(f"qnT{i}", [128, 4, 128], BF16) for i in range(2)]
    knT_2 = [c.sb(f"knT{i}", [128, 4, 128], BF16) for i in range(2)]
    knTf_2 = [c.sb(f"knTf{i}", [128, 4, 128]) for i in range(2)]
    ktok_2 = [c.sb(f"ktok{i}", [128, 4, 128], BF16) for i in range(2)]
    vtok_2 = [c.sb(f"vtok{i}", [128, 4, 128]) for i in range(2)]
    sgate = c.sb("sgate", [128, 512])
    beta = c.sb("beta", [128, 4]); lnb = c.sb("lnb", [128, 4]); spx = c.sb("spx", [128, 4]); gg = c.sb("gg", [128, 4])
    Gcol = c.sb("Gcol", [128, 4]); negG = c.sb("negG", [128, 4]); eG = c.sb("eG", [128, 4]); beG = c.sb("beG", [128, 4])
    gl = c.sb("gl", [128, 4]); biasA = c.sb("biasA", [128, 4]); wdec = c.sb("wdec", [128, 4]); GBl = c.sb("GBl", [128, 4])
    glast = c.sb("glast", [128, 4])
    gbc = [c.sb(f"gbc{h}", [128, 128]) for h in range(NH)]
    lbc = [c.sb(f"lbc{h}", [128, 128]) for h in range(NH)]
    Dq = [c.sb(f"Dq{h}", [128, 128]) for h in range(NH)]
    Dm = [c.sb(f"Dm{h}", [128, 128]) for h in range(NH)]
    DA = [c.sb(f"DA{h}", [128, 128]) for h in range(NH)]
    EGB = [c.sb(f"EGB{h}", [128, 128]) for h in range(NH)]
    Pm = [[c.sb(f"Pm{h}_{i}", [128, 128], A1_G1) for i in range(7)] for h in range(NH)]
    Pa = [[c.sb(f"Pa{h}_{i}", [128, 128], A1_G1) for i in range(2)] for h in range(NH)]
    Yf = [c.sb(f"Yf{h}", [128, 128]) for h in range(NH)]
    Yb = [c.sb(f"Yb{h}", [128, 128], A1_G1) for h in range(NH)]
    t1 = [c.sb(f"t1_{h}", [128, 128]) for h in range(NH)]
    attT = [c.sb(f"attT{h}", [128, 128], A1_G1) for h in range(NH)]
    qdT = [c.sb(f"qdT{h}", [128, 128], A1_G2) for h in range(NH)]
    kdec = [c.sb(f"kdec{h}", [128, 128], A1_G1) for h in range(NH)]
    Sst = [c.sb(f"Sst{h}", [128, 128]) for h in range(NH)]
    Sb = [c.sb(f"Sb{h}", [128, 128], A1_G2) for h in range(NH)]
    osq = c.sb("osq", [128, 128]); oss = c.sb("oss", [128, 1]); otmp = c.sb("otmp", [128, 1]); orstd = c.sb("orstd", [128, 1])
    on = c.sb("on", [128, 128])
    yt = [c.sb(f"yt{i}", [128, 512]) for i in range(2)]
    ydst = [c.dram(f"ydst{i}", None) for i in range(2)]
    P = env.P

    _pn = [0]

    def nps():
        _pn[0] += 1
        return P[_pn[0] % len(P)]

    for h in range(NH):
        k.memset("dve", Sst[h], Sst[h][:], 0.0)
        k.memset("dve", Sb[h], Sb[h][:], 0.0)
    k.memset("dve", zc, zc[:], 0.0)

    env.setup_done()
    yield
    def front(ch):
        X = xt[ch % 2]
        r0 = ch * 128
        ztok = ztok_2[ch % 2]; qnT = qnT_2[ch % 2]; knT = knT_2[ch % 2]; knTf = knTf_2[ch % 2]; ktok = ktok_2[ch % 2]; vtok = vtok_2[ch % 2]
        k.load("sp", X, X[:], D["x"], x[r0:r0 + 128, :])
        k.act(sq, sq[:], X[:], AF.Square, [X], accum=ssq[:], extra_w=[ssq])
        k.rstd_from_ssq(ssq, tmp1, rstd, 1024.0, None)
        k.stt(hn, hn[:], X[:], rstd[:], wn_t[:], ALU.mult, ALU.mult, [X, rstd, wn_t])
        pt = nps()
        ptb = pt[:].bitcast(BF16)
        for kk in range(8):
            k.tr(pt, ptb[:, kk * 128:(kk + 1) * 128], hn[:, kk * 128:(kk + 1) * 128], idb_t[:], [hn, idb_t])
        k.cp("act", hnT, hnT[:].rearrange("p a b -> p (a b)"), ptb[:, 0:1024], [pt])
        for (c0, n) in [(0, 512), (512, 8)]:
            pz = nps()
            for kk in range(8):
                k.mm(pz, pz[:, 0:n], hnT[:, kk, :], wtok_t[:, kk, c0:c0 + n], kk == 0, kk == 7, [hnT, wtok_t])
            k.cp("act", ztok, ztok[:, c0:c0 + n], pz[:, 0:n], [pz])
        for g3 in range(3):
            pf = nps()
            for j in range(4):
                jj = g3 * 4 + j
                for kk in range(8):
                    k.mm(pf, pf[:, j * 128:(j + 1) * 128], wfeat_t[:, kk, jj * 128:(jj + 1) * 128], hnT[:, kk, :],
                         kk == 0, kk == 7, [hnT, wfeat_t])
            k.cp("act" if g3 != 1 else "dve", zc, zc[:, g3 * 4:(g3 + 1) * 4, 3:131],
                 pf[:].rearrange("p (a b) -> p a b", a=4), [pf])
        for j in range(12):
            k.ts("dve", cacc, cacc[:, j, :], zc[:, j, 0:128], convw_t[:, j * 4:j * 4 + 1], None, ALU.mult, None,
                 [zc, convw_t])
            for tp in range(1, 4):
                k.stt(cacc, cacc[:, j, :], zc[:, j, tp:tp + 128], convw_t[:, j * 4 + tp:j * 4 + tp + 1], cacc[:, j, :],
                      ALU.mult, ALU.add, [zc, convw_t, cacc])
        k.cp("dve", zc, zc[:, :, 0:3], zc[:, :, 128:131], [zc])
        k.act(csil, csil[:], cacc[:], AF.Silu, [cacc])
        k.tt("pool", sq2, sq2[:], csil[:, 0:8, :], csil[:, 0:8, :], ALU.mult, [csil])
        for hf in range(2):
            pn = nps()
            k.mm(pn, pn[:, :], ones_t[:], sq2[:, hf * 4:(hf + 1) * 4, :].rearrange("p a b -> p (a b)"), True, True,
                 [ones_t, sq2])
            k.ts("dve", rn, rn[:, hf * 4:(hf + 1) * 4, :].rearrange("p a b -> p (a b)"), pn[:, :], EPS, None, ALU.add, None,
                 [pn])
        k.act(rn, rn[:], rn[:], AF.Sqrt, [rn])
        k.recip(rn, rn[:], rn[:], [rn])
        k.stt(qnT, qnT[:], csil[:, 0:4, :], 128.0 ** -0.5, rn[:, 0:4, :], ALU.mult, ALU.mult, [csil, rn])
        k.tt("dve", knTf, knTf[:], csil[:, 4:8, :], rn[:, 4:8, :], ALU.mult, [csil, rn])
        k.cp("pool", knT, knT[:], knTf[:], [knTf])
        pk = nps()
        pkb = pk[:].bitcast(BF16)
        for h in range(NH):
            k.tr(pk, pkb[:, h * 128:(h + 1) * 128], knT[:, h, :], idb_t[:], [knT, idb_t])
        k.cp("act", ktok, ktok[:].rearrange("p a b -> p (a b)"), pkb[:, 0:512], [pk])
        pv = nps()
        for h in range(NH):
            k.tr(pv, pv[:, h * 128:(h + 1) * 128], csil[:, 8 + h, :], idf_t[:], [csil, idf_t])
        k.cp("act", vtok, vtok[:].rearrange("p a b -> p (a b)"), pv[:, :], [pv])

    def rest(ch):
        Y = yt[ch % 2]
        r0 = ch * 128
        ztok = ztok_2[ch % 2]; qnT = qnT_2[ch % 2]; knT = knT_2[ch % 2]; knTf = knTf_2[ch % 2]; ktok = ktok_2[ch % 2]; vtok = vtok_2[ch % 2]
        k.act(sgate, sgate[:], ztok[:, 0:512], AF.Silu, [ztok])
        k.act(lnb, lnb[:], ztok[:, 512:516], AF.Exp, [ztok], scale=-1.0)
        k.ts("dve", beta, beta[:], lnb[:], 1.0, None, ALU.add, None, [lnb])
        k.act(lnb, lnb[:], beta[:], AF.Ln, [beta])
        k.ts("dve", lnb, lnb[:], lnb[:], -1.0, None, ALU.mult, None, [lnb])
        k.recip(beta, beta[:], beta[:], [beta])
        k.tt("dve", spx, spx[:], ztok[:, 516:520], adt_t[:, 4:8], ALU.add, [ztok, adt_t])
        k.act(spx, spx[:], spx[:], AF.Exp, [spx])
        k.act(spx, spx[:], spx[:], AF.Ln, [spx], bias=1.0)
        k.tt("dve", gg, gg[:], spx[:], negA[:], ALU.mult, [spx, negA])
        pg = nps()
        k.mm(pg, pg[:, 0:4], utri_t[:], gg[:], True, True, [utri_t, gg])
        k.cp("dve", Gcol, Gcol[:], pg[:, 0:4], [pg])
        k.ts("dve", negG, negG[:], Gcol[:], -1.0, None, ALU.mult, None, [Gcol])
        k.act(eG, eG[:], Gcol[:], AF.Exp, [Gcol])
        k.tt("dve", beG, beG[:], eG[:], beta[:], ALU.mult, [eG, beta])
        k.tt("dve", biasA, biasA[:], Gcol[:], lnb[:], ALU.add, [Gcol, lnb])
        pGB = []
        pB2 = []
        for h in range(NH):
            k.cp("dve", gbc[h], gbc[h][:], gg[:, h:h + 1].to_broadcast([128, 128]), [gg])
            k.cp("pool", lbc[h], lbc[h][:], lnb[:, h:h + 1].to_broadcast([128, 128]), [lnb])
        for h in range(NH):
            pb = nps()
            k.mm(pb, pb[:, 0:128], gbc[h][:], utri_t[:], True, True, [gbc[h], utri_t])
            k.mm(pb, pb[:, 128:256], gbc[h][:], utri_t[:], True, False, [gbc[h], utri_t])
            k.mm(pb, pb[:, 128:256], idf_t[:], masks_t[:, 0:128], False, True, [idf_t, masks_t])
            k.mm(pb, pb[:, 256:384], gbc[h][:], utri_t[:], True, False, [gbc[h], utri_t])
            k.mm(pb, pb[:, 256:384], lbc[h][:], idf_t[:], False, False, [lbc[h], idf_t])
            k.mm(pb, pb[:, 256:384], idf_t[:], masks_t[:, 128:256], False, True, [idf_t, masks_t])
            k.mm(pb, pb[:, 384:512], gbc[h][:], utri_t[:], True, False, [gbc[h], utri_t])
            k.mm(pb, pb[:, 384:512], idf_t[:], masks_t[:, 256:384], False, True, [idf_t, masks_t])
            k.act(EGB[h], EGB[h][:], pb[:, 0:128], AF.Exp, [pb])
            k.act(Dq[h], Dq[h][:], pb[:, 128:256], AF.Exp, [pb, negG], bias=negG[:, h:h + 1])
            k.act(Dm[h], Dm[h][:], pb[:, 256:384], AF.Exp, [pb, negG], bias=negG[:, h:h + 1])
            k.act(DA[h], DA[h][:], pb[:, 384:512], AF.Exp, [pb, biasA], bias=biasA[:, h:h + 1], scale=-1.0)
            k.cp("act", GBl, GBl[:, h:h + 1], pb[:, 127:128], [pb])
        k.act(glast, glast[:], GBl[:], AF.Exp, [GBl])
        for h in range(NH):
            k.act(wdec, wdec[:, h:h + 1], negG[:, h:h + 1], AF.Exp, [negG, GBl], bias=GBl[:, h:h + 1])
        for h in range(NH):
            pkk = nps()
            k.mm(pkk, pkk[:, 0:128], knT[:, h, :], knT[:, h, :], True, True, [knT])
            k.mm(pkk, pkk[:, 128:256], knT[:, h, :], qnT[:, h, :], True, True, [knT, qnT])
            k.tt("dve", Pm[h][0], Pm[h][0][:], pkk[:, 0:128], Dm[h][:], ALU.mult, [pkk, Dm[h]])
            k.tt("dve", Pa[h][0], Pa[h][0][:], pkk[:, 0:128], DA[h][:], ALU.mult, [pkk, DA[h]])
            k.tt("dve", attT[h], attT[h][:], pkk[:, 128:256], Dq[h][:], ALU.mult, [pkk, Dq[h]])
            k.tt("pool", qdT[h], qdT[h][:], qnT[:, h, :], EGB[h][:], ALU.mult, [qnT, EGB[h]])
            k.ts("dve", kdec[h], kdec[h][:], ktok[:, h, :], wdec[:, h:h + 1], None, ALU.mult, None, [ktok, wdec])
            pass
        for st in range(6):
            cur = st % 2
            nxt = 1 - cur
            for h in range(NH):
                pp = nps()
                k.mm(pp, pp[:, 0:128], Pa[h][cur][:], Pm[h][st][:], True, True, [Pa[h][cur], Pm[h][st]])
                k.mm(pp, pp[:, 128:256], Pm[h][st][:], Pa[h][cur][:], True, True, [Pa[h][cur], Pm[h][st]])
                k.cp("act", Pm[h][st + 1], Pm[h][st + 1][:], pp[:, 0:128], [pp])
                k.cp("dve", Pa[h][nxt], Pa[h][nxt][:], pp[:, 128:256], [pp])
        for h in range(NH):
            pks = nps()
            k.mm(pks, pks[:, 0:128], knTf[:, h, :], Sst[h][:], True, True, [knTf, Sst[h]])
            k.stt(t1[h], t1[h][:], pks[:, 0:128], eG[:, h:h + 1], vtok[:, h, :], ALU.mult, ALU.subtract, [pks, eG, vtok])
            k.ts("dve", Yf[h], Yf[h][:], t1[h][:], beta[:, h:h + 1], -1.0, ALU.mult, ALU.mult, [t1[h], beta])
            k.cp("act", Yb[h], Yb[h][:], Yf[h][:], [Yf[h]])
        for st in range(7):
            for h in range(NH):
                py = nps()
                k.mm(py, py[:, 0:128], Pm[h][st][:], Yb[h][:], True, True, [Pm[h][st], Yb[h]])
                k.tt("dve", Yf[h], Yf[h][:], Yf[h][:], py[:, 0:128], ALU.subtract if st == 0 else ALU.add, [Yf[h], py])
                k.cp("act", Yb[h], Yb[h][:], Yf[h][:], [Yf[h]])
        for h in range(NH):
            po = nps()
            k.mm(po, po[:, 0:128], attT[h][:], Yb[h][:], True, False, [attT[h], Yb[h]])
            k.mm(po, po[:, 0:128], qdT[h][:], Sb[h][:], False, True, [qdT[h], Sb[h]])
            pu = nps()
            k.mm(pu, pu[:, 0:128], kdec[h][:], Yb[h][:], True, True, [kdec[h], Yb[h]])
            k.stt(Sst[h], Sst[h][:], Sst[h][:], glast[:, h:h + 1], pu[:, 0:128], ALU.mult, ALU.add, [Sst[h], glast, pu])
            k.cp("act", Sb[h], Sb[h][:], Sst[h][:], [Sst[h]])
            k.act(osq, osq[:], po[:, 0:128], AF.Square, [po], accum=oss[:], extra_w=[oss])
            k.rstd_from_ssq(oss, otmp, orstd, 128.0, None)
            k.stt(on, on[:], po[:, 0:128], orstd[:], nw_t[:], ALU.mult, ALU.mult, [po, orstd, nw_t])
            k.tt("dve", Y, Y[:, h * 128:(h + 1) * 128], on[:], sgate[:, h * 128:(h + 1) * 128], ALU.mult, [on, sgate])
        c.dma("sp", ydst[ch % 2], Y, (lambda r0=r0, Y=Y: nc.sync.dma_start(out=y[r0:r0 + 128, :], in_=Y[:])))

    front(0)
    for ch in range(NCH):
        if ch + 1 < NCH:
            front(ch + 1)
        rest(ch)
    return env.end(ydst)


def inputs_A1(x, norm_w, w_in, conv_w, a_log, dt_bias, dn_norm_w):
    B, S, _ = x.shape
    identf = np.eye(128, dtype=np.float32)
    utri = np.triu(np.ones((128, 128), dtype=np.float32))
    BIG = np.float32(1.0e5)
    jj, ii = np.meshgrid(np.arange(128), np.arange(128), indexing="ij")
    masks = np.concatenate([np.where(jj > ii, -BIG, 0.0), np.where(jj >= ii, -BIG, 0.0), np.where(ii >= jj, BIG, 0.0)],
                           axis=1).astype(np.float32)
    maps = []
    for core in range(8):
        b, g = core // 2, core % 2
        hs = [4 * g + i for i in range(4)]
        def cols(base):
            return np.concatenate([np.arange(base + h * 128, base + (h + 1) * 128) for h in hs])
        feat_cols = np.concatenate([cols(0), cols(1024), cols(2048)])
        tok_cols = np.concatenate([cols(3072), np.array([4096 + h for h in hs]), np.array([4104 + h for h in hs])])
        cw = conv_w[:, feat_cols]
        convw = np.ascontiguousarray(cw.reshape(4, 12, 128).transpose(2, 1, 0).reshape(128, 48))
        adt = np.tile(np.concatenate([a_log[hs], dt_bias[hs]])[None, :], (128, 1)).astype(np.float32)
        maps.append(dict(x=np.ascontiguousarray(x[b]), wn=np.tile(norm_w[None, :], (128, 1)).astype(np.float32),
                         w_tok=np.ascontiguousarray(w_in[:, tok_cols]), w_feat=np.ascontiguousarray(w_in[:, feat_cols]),
                         convw=convw, adt=adt, nw=np.tile(dn_norm_w[None, :], (128, 1)).astype(np.float32),
                         identf=identf, utri=utri, masks=masks))
    return maps


def gather_A1(results, B, S):
    mixed = np.empty((B, S, 1024), dtype=np.float32)
    for core in range(8):
        b, g = core // 2, core % 2
        mixed[b, :, g * 512:(g + 1) * 512] = results[core]["y"]
    return mixed


def build_fused(S):
    nc = bass.Bass("TRN2", target_bir_lowering=False)
    c = Ctx(nc)
    k = K(nc, c)
    P = [c.ps(f"P{i}") for i in range(8)]
    ein = lambda n, shp: nc.dram_tensor(n, list(shp), F32, kind="ExternalInput").ap()
    x = ein("x", [S, 1024]); p0 = ein("p0", [S, 256]); p1 = ein("p1", [S, 256])
    identf = ein("identf", [128, 128]); utri = ein("utri", [128, 128]); iota = ein("iota", [128, 128])
    cosd = ein("cos", [S, 64]); sind = ein("sin", [S, 64]); masks = ein("masks", [128, 384])
    wn0 = ein("wn0", [128, 1024]); wn1 = ein("wn1", [128, 1024]); nw = ein("nw", [128, 128])
    out = nc.dram_tensor("out", [S, 1024], F32, kind="ExternalOutput").ap()
    mixed_d = nc.dram_tensor("mixed_d", [S, 1024], F32, kind="Internal").ap()
    h0_d = nc.dram_tensor("h0_d", [S, 1024], F32, kind="Internal").ap()
    ub = nc.dram_tensor("ub", [128, 128, 1024], BF16, kind="Internal").ap()
    vbd = nc.dram_tensor("vbd", [128, 128, 1024], BF16, kind="Internal").ap()
    c.begin_phase()
    _interleave([gen_A0(S, Env(nc, c, k, P[g * 4:(g + 1) * 4], tag=f"a0g{g}_", inner=True, prefix=f"g{g}_",
                               over=dict(x=x, cos=cosd, sin=sind, identf=identf, utri=utri, wn=wn0,
                                         y=mixed_d[:, g * 512:(g + 1) * 512]))) for g in range(2)])
    c.end_phase()
    build_B(S, False, env=Env(nc, c, k, P, tag="b0_", over=dict(hin=x, mixed=mixed_d, pin=p0, identf=identf, iota=iota,
                                                                y=h0_d, ub=ub, vbd=vbd)), write_y=True, write_yf=False)
    for g in range(2):
        build_A1(S, Env(nc, c, k, P, tag=f"a1g{g}_", over=dict(x=h0_d, identf=identf, utri=utri, masks=masks, wn=wn1, nw=nw,
                                                                 y=mixed_d[:, g * 512:(g + 1) * 512])))
    build_B(S, True, env=Env(nc, c, k, P, tag="b1_", over=dict(hin=h0_d, mixed=mixed_d, pin=p1, identf=identf, iota=iota,
                                                               yf=out, ub=ub, vbd=vbd)), write_y=False, write_yf=True)
    return nc


def inputs_fused(x, p, norm_mix_w, norm_ffn_w, ab_w_in, ab_conv_w, ab_b_i, ab_b_f, ab_gn_ret, ab_gn_mlstm, ab_w_out,
                 dn_w_in, dn_conv_w, dn_a_log, dn_dt_bias, dn_norm_w, dn_w_out, peer_w_q, peer_sub_keys, peer_u, peer_v,
                 ple_w_proj, ple_w_gate, ple_norm_w, final_norm_w, n_cores=8):
    B, S, _ = x.shape
    mA0 = inputs_A0(x, norm_mix_w[0], ab_w_in[0], ab_conv_w[0], ab_b_i[0], ab_b_f[0], ab_gn_ret[0], ab_gn_mlstm[0])
    mA1 = inputs_A1(x, norm_mix_w[1], dn_w_in[0], dn_conv_w[0], dn_a_log[0], dn_dt_bias[0], dn_norm_w[0])
    perm = np.concatenate([np.concatenate([np.arange(2 * g * 128, (2 * g + 2) * 128),
                                           512 + np.arange(2 * g * 128, (2 * g + 2) * 128)]) for g in range(2)])
    cB = [consts_B(np.ascontiguousarray(ab_w_out[0][perm]), ple_w_gate[0], peer_w_q[0], ple_w_proj[0], norm_ffn_w[0],
                   ple_norm_w[0], final_norm_w, peer_sub_keys[0], peer_u[0], peer_v[0]),
          consts_B(dn_w_out[0], ple_w_gate[1], peer_w_q[1], ple_w_proj[1], norm_ffn_w[1], ple_norm_w[1], final_norm_w,
                   peer_sub_keys[1], peer_u[1], peer_v[1])]
    shared = dict(identf=mA0[0]["identf"], utri=mA0[0]["utri"], iota=cB[0]["iota"], cos=mA0[0]["cos"], sin=mA0[0]["sin"],
                  masks=mA1[0]["masks"], wn0=mA0[0]["wn"], wn1=mA1[0]["wn"], nw=mA1[0]["nw"])
    maps = []
    for core in range(n_cores):
        b = core % B
        m = dict(shared, x=np.ascontiguousarray(x[b]), p0=np.ascontiguousarray(p[0][b]), p1=np.ascontiguousarray(p[1][b]))
        for g in range(2):
            for kx in ["w_tok", "w_feat", "convw", "bif", "gnw", "dmT", "xir", "zg"]:
                m[f"a0g{g}_{kx}"] = mA0[b * 2 + g][kx]
            for kx in ["w_tok", "w_feat", "convw", "adt"]:
                m[f"a1g{g}_{kx}"] = mA1[b * 2 + g][kx]
        for L in range(2):
            for kx in ["wout", "wgate", "wq", "wproj", "wnorm", "skT", "ul", "vl"]:
                m[f"b{L}_{kx}"] = cB[L][kx]
        maps.append(m)
    return maps


def kernel(x, p, norm_mix_w, norm_ffn_w, ab_w_in, ab_conv_w, ab_b_i, ab_b_f, ab_gn_ret, ab_gn_mlstm, ab_w_out,
           dn_w_in, dn_conv_w, dn_a_log, dn_dt_bias, dn_norm_w, dn_w_out, peer_w_q, peer_sub_keys, peer_u, peer_v,
           ple_w_proj, ple_w_gate, ple_norm_w, final_norm_w):
    f = lambda a: np.ascontiguousarray(np.asarray(a, dtype=np.float32))
    args = [f(a) for a in (x, p, norm_mix_w, norm_ffn_w, ab_w_in, ab_conv_w, ab_b_i, ab_b_f, ab_gn_ret, ab_gn_mlstm,
                           ab_w_out, dn_w_in, dn_conv_w, dn_a_log, dn_dt_bias, dn_norm_w, dn_w_out, peer_w_q,
                           peer_sub_keys, peer_u, peer_v, ple_w_proj, ple_w_gate, ple_norm_w, final_norm_w)]
    B, S, _ = args[0].shape
    nc = build_fused(S)
    maps = inputs_fused(*args)
    res = run_bass_kernel_spmd(nc, maps, core_ids=list(range(8))).results
    return np.stack([res[b]["out"] for b in range(B)], axis=0).astype(np.float32)
```
